# Optimizing a Trainium2 kernel written in Bass

```python
import jax, jax.numpy as jnp
from jax import lax
import numpy as np

D_MODEL = 1024
BATCH = 16
SEQ = 2048
DEPTH = 1

D_FF = 2816
MLSTM_HEADS = 4
MLSTM_DQK = 128
MLSTM_DV = 256
MLSTM_CHUNK = 64
CONV_WIDTH = 4
FOX_HEADS = 16
FOX_DH = 64
FOX_BLOCK = 128
N_MOD = 9
EPS = 1e-6
MLSTM_QK = MLSTM_HEADS * MLSTM_DQK
MLSTM_V = MLSTM_HEADS * MLSTM_DV
FOX_W = FOX_HEADS * FOX_DH
MIX_SPLITS = (MLSTM_QK, MLSTM_QK, MLSTM_V, MLSTM_V, MLSTM_HEADS, MLSTM_HEADS, FOX_W, FOX_W, FOX_W, FOX_HEADS, D_MODEL, D_MODEL)
MIX_WIDTH = 2 * MLSTM_QK + 2 * MLSTM_V + 2 * MLSTM_HEADS + 3 * FOX_W + FOX_HEADS + 2 * D_MODEL

kernel_name = "hybrid_mlstm_fox_macaron_adaln"


def _mix_offsets():
    offs = [0]
    for s in MIX_SPLITS:
        offs.append(offs[-1] + s)
    return offs


def rms_norm(x, g):
    xf = x.astype(jnp.float32)
    y = xf * lax.rsqrt(jnp.mean(xf * xf, axis=-1, keepdims=True) + EPS)
    return (y * g.astype(jnp.float32)).astype(x.dtype)


def swiglu(u, w_in, w_out):
    a, b = jnp.split(u @ w_in, 2, axis=-1)
    return (jax.nn.silu(a) * b) @ w_out


def causal_conv(x, w, b):
    y = lax.conv_general_dilated(x, w[:, None, :], window_strides=(1,), padding=[(CONV_WIDTH - 1, 0)],
                                 dimension_numbers=('NWC', 'WIO', 'NWC'), feature_group_count=x.shape[-1])
    return y + b


def mlstm_chunkwise(q, k, v, i_pre, f_pre):
    B, S, H, _ = q.shape
    L = MLSTM_CHUNK
    nc = S // L
    f32 = jnp.float32

    def to_chunks(t):
        t = jnp.moveaxis(t.astype(f32), 2, 1)
        t = t.reshape((B, H, nc, L) + t.shape[3:])
        return jnp.moveaxis(t, 2, 0)

    qc = to_chunks(q)
    kc = to_chunks(k) * (MLSTM_DQK ** -0.5)
    vc = to_chunks(v)
    ic = to_chunks(i_pre)
    lfc = to_chunks(jax.nn.log_sigmoid(f_pre.astype(f32)))
    causal = jnp.tril(jnp.ones((L, L), dtype=bool))

    def step(carry, inp):
        C, n, m = carry
        qt, kt, vt, it, lft = inp
        b = jnp.cumsum(lft, axis=-1)
        log_d = jnp.where(causal, b[..., :, None] - b[..., None, :] + it[..., None, :], -jnp.inf)
        log_inter = b + m[..., None]
        m_t = jnp.maximum(log_inter, jnp.max(log_d, axis=-1))
        w_inter = jnp.exp(log_inter - m_t)
        s = jnp.einsum('bhtd,bhsd->bhts', qt, kt) * jnp.exp(log_d - m_t[..., None])
        num = w_inter[..., None] * jnp.einsum('bhtd,bhde->bhte', qt, C) + jnp.einsum('bhts,bhse->bhte', s, vt)
        den = w_inter * jnp.einsum('bhtd,bhd->bht', qt, n) + jnp.sum(s, axis=-1)
        h = num / jnp.maximum(jnp.abs(den), jnp.exp(-m_t))[..., None]
        b_last = b[..., -1]
        log_w = b_last[..., None] - b + it
        m_new = jnp.maximum(b_last + m, jnp.max(log_w, axis=-1))
        w_k = jnp.exp(log_w - m_new[..., None])
        decay = jnp.exp(b_last + m - m_new)
        C_new = decay[..., None, None] * C + jnp.einsum('bhs,bhsd,bhse->bhde', w_k, kt, vt)
        n_new = decay[..., None] * n + jnp.einsum('bhs,bhsd->bhd', w_k, kt)
        return (C_new, n_new, m_new), h

    init = (jnp.zeros((B, H, MLSTM_DQK, MLSTM_DV), f32), jnp.zeros((B, H, MLSTM_DQK), f32), jnp.zeros((B, H), f32))
    _, h = lax.scan(step, init, (qc, kc, vc, ic, lfc))
    h = jnp.moveaxis(h, 0, 2).reshape(B, H, S, MLSTM_DV)
    return jnp.moveaxis(h, 1, 2)


def forgetting_attention(q, k, v, f_pre, q_g, k_g):
    B, S, H, Dh = q.shape
    f32 = jnp.float32
    q = jnp.moveaxis(rms_norm(q, q_g), 1, 2)
    k = jnp.moveaxis(rms_norm(k, k_g), 1, 2)
    v = jnp.moveaxis(v, 1, 2)
    F = jnp.moveaxis(jnp.cumsum(jax.nn.log_sigmoid(f_pre.astype(f32)), axis=1), 1, 2)
    nb = S // FOX_BLOCK
    qb = jnp.moveaxis(q.reshape(B, H, nb, FOX_BLOCK, Dh), 2, 0)
    Fb = jnp.moveaxis(F.reshape(B, H, nb, FOX_BLOCK), 2, 0)
    k_pos = jnp.arange(S)
    scale = Dh ** -0.5

    def block(args):
        q_blk, F_blk, blk_idx = args
        q_pos = blk_idx * FOX_BLOCK + jnp.arange(FOX_BLOCK)
        logits = jnp.einsum('bhqd,bhkd->bhqk', q_blk, k).astype(f32) * scale + (F_blk[..., :, None] - F[..., None, :])
        logits = jnp.where(k_pos[None, :] <= q_pos[:, None], logits, -jnp.inf)
        p = jax.nn.softmax(logits, axis=-1)
        return jnp.einsum('bhqk,bhkd->bhqd', p.astype(v.dtype), v)

    out = lax.map(block, (qb, Fb, jnp.arange(nb)))
    out = jnp.moveaxis(out, 0, 2).reshape(B, H, S, Dh)
    return jnp.moveaxis(out, 1, 2).reshape(B, S, H * Dh)


def hybrid_layer(x, c, w_ada, b_ada, ffn1_norm_g, ffn1_w_in, ffn1_w_out, mix_norm_g, w_mix, b_mix,
                 conv_w, conv_b, mlstm_norm_g, fox_q_norm_g, fox_k_norm_g, w_branch_a, w_branch_b, w_out,
                 ffn2_norm_g, ffn2_w_in, ffn2_w_out):
    B, S, _ = x.shape
    mod = jax.nn.silu(c) @ w_ada + b_ada
    sh1, sc1, g1, sh2, sc2, g2, sh3, sc3, g3 = [m[:, None, :] for m in jnp.split(mod, N_MOD, axis=-1)]

    u = rms_norm(x, ffn1_norm_g) * (1 + sc1) + sh1
    x = x + 0.5 * g1 * swiglu(u, ffn1_w_in, ffn1_w_out)

    u = rms_norm(x, mix_norm_g) * (1 + sc2) + sh2
    z = u @ w_mix + b_mix
    offs = _mix_offsets()
    q_m, k_m, v_m, o_m, i_m, f_m, q_f, k_f, v_f, f_f, g_a, g_b = [z[..., offs[j]:offs[j + 1]] for j in range(len(MIX_SPLITS))]

    qk_m = jax.nn.silu(causal_conv(jnp.concatenate([q_m, k_m], axis=-1), conv_w, conv_b))
    q_m, k_m = jnp.split(qk_m, 2, axis=-1)
    h_m = mlstm_chunkwise(q_m.reshape(B, S, MLSTM_HEADS, MLSTM_DQK), k_m.reshape(B, S, MLSTM_HEADS, MLSTM_DQK),
                          v_m.reshape(B, S, MLSTM_HEADS, MLSTM_DV), i_m, f_m)
    y_a = jax.nn.sigmoid(o_m) * rms_norm(h_m, mlstm_norm_g).reshape(B, S, MLSTM_V).astype(x.dtype)

    y_b = forgetting_attention(q_f.reshape(B, S, FOX_HEADS, FOX_DH), k_f.reshape(B, S, FOX_HEADS, FOX_DH),
                               v_f.reshape(B, S, FOX_HEADS, FOX_DH), f_f, fox_q_norm_g, fox_k_norm_g)

    merged = jax.nn.sigmoid(g_a) * (y_a @ w_branch_a) + jax.nn.sigmoid(g_b) * (y_b @ w_branch_b)
    x = x + g2 * (merged @ w_out)

    u = rms_norm(x, ffn2_norm_g) * (1 + sc3) + sh3
    x = x + 0.5 * g3 * swiglu(u, ffn2_w_in, ffn2_w_out)
    return x


def setup_inputs(seed: int = 0) -> dict:
    key = jax.random.key(seed)
    ks = jax.random.split(key, 24)
    f32 = jnp.float32
    L = DEPTH
    D = D_MODEL

    def nrm(k, shape, scale):
        return jax.random.normal(k, shape, f32) * scale

    def gain(k, shape):
        return 1.0 + 0.05 * jax.random.normal(k, shape, f32)

    offs = _mix_offsets()
    b_mix = nrm(ks[9], (L, MIX_WIDTH), 0.02)
    b_mix = b_mix.at[:, offs[5]:offs[6]].add(jnp.linspace(3.0, 6.0, MLSTM_HEADS))
    b_mix = b_mix.at[:, offs[9]:offs[10]].add(jnp.linspace(2.0, 7.0, FOX_HEADS))
    return {
        "x": nrm(ks[0], (BATCH, SEQ, D), 1.0),
        "c": nrm(ks[1], (BATCH, D), 1.0),
        "w_ada": nrm(ks[2], (L, D, N_MOD * D), 0.5 * D ** -0.5),
        "b_ada": nrm(ks[3], (L, N_MOD * D), 0.02),
        "ffn1_norm_g": gain(ks[4], (L, D)),
        "ffn1_w_in": nrm(ks[5], (L, D, 2 * D_FF), D ** -0.5),
        "ffn1_w_out": nrm(ks[6], (L, D_FF, D), D_FF ** -0.5),
        "mix_norm_g": gain(ks[7], (L, D)),
        "w_mix": nrm(ks[8], (L, D, MIX_WIDTH), D ** -0.5),
        "b_mix": b_mix,
        "conv_w": nrm(ks[10], (L, CONV_WIDTH, 2 * MLSTM_QK), CONV_WIDTH ** -0.5),
        "conv_b": nrm(ks[11], (L, 2 * MLSTM_QK), 0.02),
        "mlstm_norm_g": gain(ks[12], (L, MLSTM_HEADS, MLSTM_DV)),
        "fox_q_norm_g": gain(ks[13], (L, FOX_HEADS, FOX_DH)),
        "fox_k_norm_g": gain(ks[14], (L, FOX_HEADS, FOX_DH)),
        "w_branch_a": nrm(ks[15], (L, MLSTM_V, D), MLSTM_V ** -0.5),
        "w_branch_b": nrm(ks[16], (L, FOX_W, D), FOX_W ** -0.5),
        "w_out": nrm(ks[17], (L, D, D), D ** -0.5),
        "ffn2_norm_g": gain(ks[18], (L, D)),
        "ffn2_w_in": nrm(ks[19], (L, D, 2 * D_FF), D ** -0.5),
        "ffn2_w_out": nrm(ks[20], (L, D_FF, D), D_FF ** -0.5),
    }


def reference(x, c, w_ada, b_ada, ffn1_norm_g, ffn1_w_in, ffn1_w_out, mix_norm_g, w_mix, b_mix,
              conv_w, conv_b, mlstm_norm_g, fox_q_norm_g, fox_k_norm_g, w_branch_a, w_branch_b, w_out,
              ffn2_norm_g, ffn2_w_in, ffn2_w_out):
    for l in range(DEPTH):
        x = hybrid_layer(x, c, w_ada[l], b_ada[l], ffn1_norm_g[l], ffn1_w_in[l], ffn1_w_out[l], mix_norm_g[l],
                         w_mix[l], b_mix[l], conv_w[l], conv_b[l], mlstm_norm_g[l], fox_q_norm_g[l],
                         fox_k_norm_g[l], w_branch_a[l], w_branch_b[l], w_out[l], ffn2_norm_g[l],
                         ffn2_w_in[l], ffn2_w_out[l])
    return x
```

```python
import numpy as np
import concourse.bass as bass
import concourse.mybir as mybir
from concourse.alu_op_type import AluOpType as ALU
from concourse.bass_utils import run_bass_kernel_spmd

F32 = mybir.dt.float32
BF16 = mybir.dt.bfloat16
AF = mybir.ActivationFunctionType
AX = mybir.AxisListType

D = 1024
S = 2048
BL = 2
T = BL * S
DFF = 2816
NJ = DFF // 128
NMOD = 9
MIXW = 8216
EPS = 1e-6
NCORES = 8
NEG = -1.0e30
import os
MIDPRE = os.environ.get('MIDPRE', '1') == '1'

O_QM, O_KM, O_VM, O_OM, O_IM, O_FM = 0, 512, 1024, 2048, 3072, 3076
O_QF, O_KF, O_VF, O_FF, O_GA, O_GB = 3080, 4104, 5128, 6152, 6168, 7192


class Buf:
    __slots__ = ("w", "weng", "rs")

    def __init__(self):
        self.w = None
        self.weng = None
        self.rs = []


class Sched:
    def __init__(self, nc):
        self.nc = nc
        self.ops = {k: [] for k in ("pe", "act", "dve", "pool", "sp")}
        self.sems = []
        self.semval = []
        self.esem = {}
        for k in self.ops:
            self.esem[k] = self._newsem("e_" + k)
        self.known = {k: {} for k in self.ops}
        self.log = {k: [] for k in self.ops}
        self.dring = {}
        self.dcnt = {}
        for q, n in (("sp", 16), ("pool", 8), ("act", 4)):
            self.dring[q] = [self._newsem("d_%s%d" % (q, i)) for i in range(n)]
            self.dcnt[q] = 0
        self.dlast = {}

    def _newsem(self, name):
        self.sems.append(self.nc.alloc_semaphore(name))
        self.semval.append(0)
        return len(self.sems) - 1

    def need(self, eng, ev):
        if ev is None:
            return
        si, val = ev
        if self.known[eng].get(si, 0) >= val:
            return
        self.known[eng][si] = val
        sem = self.sems[si]
        self.ops[eng].append(lambda e, sem=sem, val=val: e.wait_ge(sem, val))
        self.log[eng].append(("w", si, val))

    def emit(self, eng, fn, reads=(), writes=(), inc=True):
        for b in reads:
            if b.w is not None and not (b.weng == eng and eng == "pe"):
                self.need(eng, b.w)
        for b in writes:
            for (ev, re) in b.rs:
                if re != eng or eng != "pe":
                    self.need(eng, ev)
            if b.w is not None and (b.weng != eng or eng != "pe"):
                self.need(eng, b.w)
        si = self.esem[eng]
        ev = (si, self.semval[si] + 1)
        if inc:
            self.semval[si] += 1
            sem = self.sems[si]
            self.ops[eng].append(lambda e, fn=fn, sem=sem: fn(e).then_inc(sem, 1))
            self.log[eng].append(("i", si, 1))
        else:
            self.ops[eng].append(lambda e, fn=fn: fn(e))
        for b in reads:
            b.rs.append((ev, eng))
        for b in writes:
            b.w = ev
            b.weng = eng
            b.rs = []
        return ev

    def dma(self, q, fn, reads=(), writes=()):
        ring = self.dring[q]
        k = self.dcnt[q] % len(ring)
        self.dcnt[q] += 1
        si = ring[k]
        if self.semval[si] > 0:
            self.need(q, (si, self.semval[si]))
        for b in reads:
            if b.w is not None:
                self.need(q, b.w)
        for b in writes:
            for (ev, re) in b.rs:
                self.need(q, ev)
            if b.w is not None:
                self.need(q, b.w)
        self.semval[si] += 16
        ev = (si, self.semval[si])
        sem = self.sems[si]
        self.ops[q].append(lambda e, fn=fn, sem=sem: fn(e).then_inc(sem, 16))
        self.log[q].append(("i", si, 16))
        for b in reads:
            b.rs.append((ev, "dma"))
        for b in writes:
            b.w = ev
            b.weng = "dma"
            b.rs = []
        return ev

    def barrier(self):
        for eng in self.ops:
            for si in range(len(self.sems)):
                if self.semval[si] > 0:
                    self.need(eng, (si, self.semval[si]))

    def check_deadlock(self):
        val = [0] * len(self.sems)
        pos = {k: 0 for k in self.log}
        while True:
            prog = False
            for k, lg in self.log.items():
                while pos[k] < len(lg):
                    op = lg[pos[k]]
                    if op[0] == "w":
                        if val[op[1]] >= op[2]:
                            pos[k] += 1; prog = True
                        else:
                            break
                    else:
                        val[op[1]] += op[2]; pos[k] += 1; prog = True
            if all(pos[k] == len(self.log[k]) for k in self.log):
                return None
            if not prog:
                return {k: (pos[k], len(self.log[k]), self.log[k][pos[k]] if pos[k] < len(self.log[k]) else None) for k in self.log}

    def run(self, block):
        def mk(name):
            def f(e):
                for op in self.ops[name]:
                    op(e)
            return f
        block.tensor(mk("pe"))
        block.scalar(mk("act"))
        block.vector(mk("dve"))
        block.gpsimd(mk("pool"))
        block.sync(mk("sp"))


class Arena:
    def __init__(self, nc):
        self.nc = nc
        self.base = ((nc.sbuf_base + 63) // 64) * 64
        self.top = nc.sbuf_top
        self.cur = self.base
        self.n = 0

    def mark(self):
        return self.cur

    def reset(self, m):
        self.cur = m

    def tile(self, shape, dtype, name="t"):
        nbytes = int(np.prod(shape[1:])) * (2 if dtype == BF16 else 4)
        nbytes = ((nbytes + 63) // 64) * 64
        off = self.cur
        self.cur += nbytes
        assert self.cur <= self.top, "SBUF overflow %d > %d (%s)" % (self.cur, self.top, name)
        self.n += 1
        return self.nc.alloc_sbuf_tensor_at("%s_%d" % (name, self.n), list(shape), dtype, offset=off)


def build_program(debug=None):
    nc = bass.Bass("TRN2", target_bir_lowering=False)
    dt = {}

    def din(name, shape, dtype=F32):
        dt[name] = nc.dram_tensor(name, list(shape), dtype, kind="ExternalInput").ap()
        return dt[name]

    x_in = din("x", [T, D])
    c_in = din("c", [BL, D])
    w_ada = din("w_ada", [D, NMOD * D])
    b_ada = din("b_ada", [NMOD * D])
    g_f1 = din("ffn1_norm_g", [D])
    w1_in = din("ffn1_w_in", [D, 2 * DFF])
    w1_out = din("ffn1_w_out", [DFF, D])
    g_mix = din("mix_norm_g", [D])
    w_mix = din("w_mix", [D, MIXW])
    b_mix = din("b_mix", [MIXW])
    conv_w = din("conv_w", [4, D])
    conv_b = din("conv_b", [D])
    g_ml = din("mlstm_norm_g", [D])
    g_fq = din("fox_q_norm_g", [D])
    g_fk = din("fox_k_norm_g", [D])
    w_ba = din("w_branch_a", [D, D])
    w_bb = din("w_branch_b", [D, D])
    w_o = din("w_out", [D, D])
    g_f2 = din("ffn2_norm_g", [D])
    w2_in = din("ffn2_w_in", [D, 2 * DFF])
    w2_out = din("ffn2_w_out", [DFF, D])
    cst = din("consts", [128, 640])
    y_out = nc.dram_tensor("y", [T, D], F32, kind="ExternalOutput").ap()

    def scratch(name, shape, dtype):
        kind = "ExternalOutput" if (debug and name in debug) else "Internal"
        return nc.dram_tensor(name, list(shape), dtype, kind=kind).ap()

    x1s = scratch("x1s", [8, 128, T], F32)

    sc = Sched(nc)
    ar = Arena(nc)

    banks = [nc.alloc_psum_tensor("bank%d" % i, [128, 512], F32) for i in range(8)]
    bbuf = [Buf() for _ in range(8)]

    ident = ar.tile([128, 128], F32, "ident")
    maskadd = ar.tile([128, 128], F32, "maskadd")
    ones = ar.tile([128, 128], F32, "ones")
    bones = ar.tile([128, 128], F32, "bones")
    cols1 = ar.tile([128, 128], F32, "cols1")
    cols2 = ar.tile([128, 128], F32, "cols2")
    modT = ar.tile([128, 72, 2], F32, "modT")
    coef = ar.tile([128, 9, 8, 2], F32, "coef")
    B_const = Buf()
    B_cols = Buf()
    B_mod = Buf()
    B_coef = Buf()

    sc.dma("sp", lambda e: e.dma_start(out=ident[:], in_=cst[:, 0:128]), writes=[B_const])
    sc.dma("sp", lambda e: e.dma_start(out=maskadd[:], in_=cst[:, 128:256]), writes=[B_const])
    sc.dma("sp", lambda e: e.dma_start(out=ones[:], in_=cst[:, 256:384]), writes=[B_const])
    sc.dma("sp", lambda e: e.dma_start(out=bones[:], in_=cst[:, 384:512]), writes=[B_const])

    pmark = ar.mark()

    rows1 = ar.tile([128, 128], F32, "rows1")
    rows2 = ar.tile([128, 128], F32, "rows2")
    B_rows = Buf()
    sc.emit("pool", lambda e: e.memset(rows2[:], 0.0), writes=[B_rows])

    def rowdma(dst, r0, nr, src1d, off):
        sc.dma("sp", lambda e: e.dma_start(
            out=dst[r0:r0 + nr, :],
            in_=src1d[off:off + nr * 128].rearrange("(r p) -> r p", p=128)), writes=[B_rows])

    rowdma(rows1, 0, 72, b_ada, 0)
    rowdma(rows1, 72, 8, g_f1, 0)
    rowdma(rows1, 80, 8, g_mix, 0)
    rowdma(rows1, 88, 8, g_f2, 0)
    for j in range(4):
        sc.dma("sp", lambda e, j=j: e.dma_start(
            out=rows1[96 + 8 * j:104 + 8 * j, :],
            in_=conv_w[j, :].rearrange("(r p) -> r p", p=128)), writes=[B_rows])
    rowdma(rows2, 0, 8, conv_b, 0)
    rowdma(rows2, 8, 24, b_mix, 0)
    rowdma(rows2, 32, 24, b_mix, O_QF)
    rowdma(rows2, 56, 16, b_mix, O_GA)
    rowdma(rows2, 72, 8, g_fq, 0)
    rowdma(rows2, 80, 8, g_fk, 0)
    for b in range(BL):
        sc.dma("sp", lambda e, b=b: e.dma_start(
            out=rows2[96 + 8 * b:104 + 8 * b, :],
            in_=c_in[b, :].rearrange("(r p) -> r p", p=128)), writes=[B_rows])

    sc.emit("pe", lambda e: e.transpose(out=banks[0][:, 0:128], in_=rows1[:], identity=ident[:]),
            reads=[B_rows, B_const], writes=[bbuf[0]])
    sc.emit("pe", lambda e: e.transpose(out=banks[0][:, 128:256], in_=rows2[:], identity=ident[:]),
            reads=[B_rows, B_const], writes=[bbuf[0]])
    sc.emit("dve", lambda e: e.tensor_copy(out=cols1[:], in_=banks[0][:, 0:128]), reads=[bbuf[0]], writes=[B_cols])
    sc.emit("dve", lambda e: e.tensor_copy(out=cols2[:], in_=banks[0][:, 128:256]), reads=[bbuf[0]], writes=[B_cols])

    scT = ar.tile([128, BL, 8], BF16, "scT")
    B_scT = Buf()
    sc.emit("act", lambda e: e.activation(out=scT[:].rearrange("p b c -> p (b c)"), in_=cols2[:, 96:112], func=AF.Silu),
            reads=[B_cols], writes=[B_scT])

    WB = 1152
    wa = [ar.tile([128, 8, WB], BF16, "wada%d" % i) for i in range(2)]
    B_wa = [Buf(), Buf()]
    w_ada_v = w_ada.rearrange("(c p) n -> p c n", p=128)
    for blk in range(8):
        t = wa[blk % 2]
        sc.dma("pool", lambda e, t=t, blk=blk: e.dma_start(out=t[:], in_=w_ada_v[:, :, blk * WB:(blk + 1) * WB]),
               writes=[B_wa[blk % 2]])
        for jj in range(9):
            j = blk * 9 + jj
            for cc in range(8):
                sc.emit("pe", lambda e, t=t, jj=jj, cc=cc, j=j: e.matmul(
                    banks[1][:, 2 * j:2 * j + 2], lhsT=t[:, cc, jj * 128:(jj + 1) * 128], rhs=scT[:, :, cc],
                    start=(cc == 0), stop=(cc == 7)),
                    reads=[B_wa[blk % 2], B_scT], writes=[bbuf[1]], inc=(cc == 7))
    pm = banks[1][:, 0:144].rearrange("p (j b) -> p j b", b=2)
    for b in range(BL):
        sc.emit("dve", lambda e, b=b: e.tensor_tensor(out=modT[:, :, b], in0=pm[:, :, b], in1=cols1[:, 0:72], op=ALU.add),
                reads=[bbuf[1], B_cols], writes=[B_mod])
    for k, gcol in enumerate((72, 80, 88)):
        for b in range(BL):
            sh = modT[:, (3 * k) * 8:(3 * k) * 8 + 8, b]
            scl = modT[:, (3 * k + 1) * 8:(3 * k + 1) * 8 + 8, b]
            gg = modT[:, (3 * k + 2) * 8:(3 * k + 2) * 8 + 8, b]
            sc.emit("dve", lambda e, k=k, b=b, scl=scl, gcol=gcol: e.scalar_tensor_tensor(
                out=coef[:, 3 * k, :, b], in0=scl, scalar=1.0, in1=cols1[:, gcol:gcol + 8], op0=ALU.add, op1=ALU.mult),
                reads=[B_mod, B_cols], writes=[B_coef])
            sc.emit("dve", lambda e, k=k, b=b, sh=sh: e.tensor_copy(out=coef[:, 3 * k + 1, :, b], in_=sh),
                    reads=[B_mod], writes=[B_coef])
            gmul = 1.0 if k == 1 else 0.5
            sc.emit("dve", lambda e, k=k, b=b, gg=gg, gmul=gmul: e.tensor_scalar(
                out=coef[:, 3 * k + 2, :, b], in0=gg, scalar1=gmul, scalar2=None, op0=ALU.mult),
                reads=[B_mod], writes=[B_coef])

    sc.barrier()
    ar.reset(pmark)

    NT = 256
    NTILES = T // NT


    def norm_mod(k, xt, Bx, sq, Bsq, stt_, Bstt, rstd_, Brstd, ut, But, b, NT, fp32_u=False):
        sc.emit("act", lambda e: e.activation(
            out=sq[:].rearrange("p c n -> p (c n)"), in_=xt[:].rearrange("p c n -> p (c n)"), func=AF.Square),
            reads=[Bx], writes=[Bsq])
        for c in range(8):
            sc.emit("pe", lambda e, c=c: e.matmul(banks[4][:, 0:NT], lhsT=ones[:], rhs=sq[:, c, :],
                                                   start=(c == 0), stop=(c == 7)),
                    reads=[Bsq, B_const], writes=[bbuf[4]], inc=(c == 7))
        sc.emit("act", lambda e: e.activation(out=stt_[:], in_=banks[4][:, 0:NT], func=AF.Sqrt,
                                               scale=1.0 / D, bias=epsb[:]),
                reads=[bbuf[4], B_const2], writes=[Bstt])
        sc.emit("dve", lambda e: e.reciprocal(out=rstd_[:], in_=stt_[:]), reads=[Bstt], writes=[Brstd])
        sc.emit("dve", lambda e: e.tensor_tensor(out=sq[:], in0=xt[:], in1=rstd_[:].unsqueeze(1).to_broadcast([128, 8, NT]), op=ALU.mult),
                reads=[Bx, Brstd], writes=[Bsq])
        for c in range(8):
            sc.emit("act", lambda e, c=c: e.activation(
                out=(sq[:, c, :] if fp32_u else ut[:, c, :]), in_=sq[:, c, :], func=AF.Identity,
                scale=coef[:, 3 * k, c, b:b + 1], bias=coef[:, 3 * k + 1, c, b:b + 1]),
                reads=[Bsq, B_coef], writes=([Bsq] if fp32_u else [But]))
        if fp32_u:
            sc.emit("dve", lambda e: e.tensor_copy(out=ut[:], in_=sq[:]), reads=[Bsq], writes=[But])

    def ffn_phase(k, w_in_d, w_out_d, src_tokmajor, dst_tokmajor, src_fm, dst_fm):
        m0 = ar.mark()
        w_in = ar.tile([128, 8, 2 * DFF], BF16, "w_in")
        w_out = ar.tile([128, NJ, D], BF16, "w_out")
        B_win = [Buf() for _ in range(4)]
        B_wout = [Buf() for _ in range(2)]
        w_in_v = w_in_d.rearrange("(c p) n -> p c n", p=128)
        w_out_v = w_out_d.rearrange("(c p) n -> p c n", p=128)
        CW = 2 * DFF // 4
        for q in range(4):
            sc.dma("pool", lambda e, q=q: e.dma_start(out=w_in[:, :, q * CW:(q + 1) * CW], in_=w_in_v[:, :, q * CW:(q + 1) * CW]),
                   writes=[B_win[q]])
        for q in range(2):
            sc.dma("pool", lambda e, q=q: e.dma_start(out=w_out[:, q * 11:(q + 1) * 11, :], in_=w_out_v[:, q * 11:(q + 1) * 11, :]),
                   writes=[B_wout[q]])
        xin = [ar.tile([128, 2, D], F32, "xin%d" % i) for i in range(2)] if (src_tokmajor is not None or dst_tokmajor is not None) else None
        xT = [ar.tile([128, 8, NT], F32, "xT%d" % i) for i in range(2)]
        sq = ar.tile([128, 8, NT], F32, "sq")
        uT = [ar.tile([128, 8, NT], BF16, "uT%d" % i) for i in range(2)]
        g = ar.tile([128, NJ, NT], BF16, "g")
        stt = [ar.tile([128, NT], F32, "stt%d" % i) for i in range(2)]
        rstd = [ar.tile([128, NT], F32, "rstd%d" % i) for i in range(2)]
        sl = [ar.tile([128, NT], F32, "sl%d" % i) for i in range(2)]
        B_xin = [Buf(), Buf()]
        B_xT = [Buf(), Buf()]
        B_sq = Buf()
        B_uT = [Buf(), Buf()]
        B_g = [Buf() for _ in range(NJ)]
        B_stt = [Buf(), Buf()]
        B_rstd = [Buf(), Buf()]
        B_sl = [Buf(), Buf()]
        slc = 0
        def pre(it):
            nonlocal slc
            p = it % 2
            b = (it * NT) // S
            t0 = it * NT
            xt = xT[p]
            if src_tokmajor is not None:
                xi = xin[p]
                sc.dma("sp", lambda e, xi=xi, t0=t0: e.dma_start(
                    out=xi[:], in_=src_tokmajor[t0:t0 + NT, :].rearrange("(k p) d -> p k d", p=128)),
                    writes=[B_xin[p]])
                for c2 in range(4):
                    bk = 2 + (c2 % 2)
                    for cc in range(2):
                        c = 2 * c2 + cc
                        for kb in range(2):
                            sc.emit("pe", lambda e, xi=xi, bk=bk, cc=cc, kb=kb, c=c: e.transpose(
                                out=banks[bk][:, cc * 256 + kb * 128: cc * 256 + kb * 128 + 128],
                                in_=xi[:, kb, c * 128:(c + 1) * 128], identity=ident[:]),
                                reads=[B_xin[p], B_const], writes=[bbuf[bk]], inc=(cc == 1 and kb == 1))
                    eng = "act" if c2 % 2 == 0 else "dve"
                    if eng == "act":
                        sc.emit("act", lambda e, xt=xt, bk=bk, c2=c2: e.copy(
                            out=xt[:, 2 * c2:2 * c2 + 2, :].rearrange("p c n -> p (c n)"), in_=banks[bk][:, :]),
                            reads=[bbuf[bk]], writes=[B_xT[p]])
                    else:
                        sc.emit("dve", lambda e, xt=xt, bk=bk, c2=c2: e.tensor_copy(
                            out=xt[:, 2 * c2:2 * c2 + 2, :].rearrange("p c n -> p (c n)"), in_=banks[bk][:, :]),
                            reads=[bbuf[bk]], writes=[B_xT[p]])
            else:
                sc.dma("sp", lambda e, xt=xt, t0=t0: e.dma_start(
                    out=xt[:], in_=src_fm[:, :, t0:t0 + NT].rearrange("c p n -> p c n")), writes=[B_xT[p]])
            norm_mod(k, xt, B_xT[p], sq, B_sq, stt[p], B_stt[p], rstd[p], B_rstd[p], uT[p], B_uT[p], b, NT)
        def main(it):
            nonlocal slc
            p = it % 2
            b = (it * NT) // S
            t0 = it * NT
            xt = xT[p]
            for j in range(NJ):
                bk = 5 + (j % 2)
                for half in range(2):
                    col0 = half * DFF + j * 128
                    q = col0 // CW
                    assert (col0 + 127) // CW == q
                    for c in range(8):
                        sc.emit("pe", lambda e, bk=bk, half=half, c=c, col0=col0, p=p: e.matmul(
                            banks[bk][:, half * NT:(half + 1) * NT], lhsT=w_in[:, c, col0:col0 + 128], rhs=uT[p][:, c, :],
                            start=(c == 0), stop=(c == 7)),
                            reads=[B_win[q], B_uT[p]], writes=[bbuf[bk]], inc=(c == 7 and half == 1))
                s_ = sl[slc % 2]
                bs = B_sl[slc % 2]
                slc += 1
                sc.emit("act", lambda e, bk=bk, s_=s_: e.activation(out=s_[:], in_=banks[bk][:, 0:NT], func=AF.Silu),
                        reads=[bbuf[bk]], writes=[bs])
                sc.emit("dve", lambda e, bk=bk, s_=s_, j=j: e.tensor_tensor(out=g[:, j, :], in0=s_[:], in1=banks[bk][:, NT:2 * NT], op=ALU.mult),
                        reads=[bbuf[bk], bs], writes=[B_g[j]])
            for c2 in range(4):
                bk = 2 + (c2 % 2)
                for cc in range(2):
                    c = 2 * c2 + cc
                    for j in range(NJ):
                        sc.emit("pe", lambda e, bk=bk, cc=cc, c=c, j=j: e.matmul(
                            banks[bk][:, cc * NT:(cc + 1) * NT], lhsT=w_out[:, j, c * 128:(c + 1) * 128], rhs=g[:, j, :],
                            start=(j == 0), stop=(j == NJ - 1)),
                            reads=[B_wout[j // 11], B_g[j]], writes=[bbuf[bk]], inc=(j == NJ - 1 and cc == 1))
                for cc in range(2):
                    c = 2 * c2 + cc
                    sc.emit("dve", lambda e, bk=bk, cc=cc, c=c, xt=xt, b=b: e.scalar_tensor_tensor(
                        out=xt[:, c, :], in0=banks[bk][:, cc * NT:(cc + 1) * NT], scalar=coef[:, 3 * k + 2, c, b:b + 1],
                        in1=xt[:, c, :], op0=ALU.mult, op1=ALU.add),
                        reads=[bbuf[bk], B_coef, B_xT[p]], writes=[B_xT[p]])
            if dst_fm is not None:
                sc.dma("pool", lambda e, xt=xt, t0=t0: e.dma_start(
                    out=dst_fm[:, :, t0:t0 + NT].rearrange("c p n -> p c n"), in_=xt[:]), reads=[B_xT[p]])
            else:
                xi = xin[p]
                for kb in range(2):
                    for c4 in range(2):
                        bk = 2 + ((kb * 2 + c4) % 2)
                        for cq in range(4):
                            c = c4 * 4 + cq
                            sc.emit("pe", lambda e, xt=xt, bk=bk, cq=cq, kb=kb, c=c: e.transpose(
                                out=banks[bk][:, cq * 128:(cq + 1) * 128],
                                in_=xt[:, c, kb * 128:(kb + 1) * 128], identity=ident[:]),
                                reads=[B_xT[p], B_const], writes=[bbuf[bk]], inc=(cq == 3))
                        if c4 == 0:
                            sc.emit("act", lambda e, xi=xi, bk=bk, kb=kb, c4=c4: e.copy(
                                out=xi[:, kb, c4 * 512:(c4 + 1) * 512], in_=banks[bk][:, :]),
                                reads=[bbuf[bk]], writes=[B_xin[p]])
                        else:
                            sc.emit("dve", lambda e, xi=xi, bk=bk, kb=kb, c4=c4: e.tensor_copy(
                                out=xi[:, kb, c4 * 512:(c4 + 1) * 512], in_=banks[bk][:, :]),
                                reads=[bbuf[bk]], writes=[B_xin[p]])
                out_evs.append(sc.dma("pool", lambda e, xi=xi, t0=t0: e.dma_start(
                    out=dst_tokmajor[t0:t0 + NT, :].rearrange("(k p) d -> p k d", p=128), in_=xi[:]),
                    reads=[B_xin[p]]))
        NTL_ = NTILES
        pre(0)
        for it in range(NTL_):
            if it + 1 < NTL_:
                pre(it + 1)
            main(it)
        sc.barrier()
        ar.reset(m0)

    out_evs = []
    epsb = ar.tile([128, 1], F32, "epsb")
    B_const2 = Buf()
    sc.emit("pool", lambda e: e.memset(epsb[:], EPS), writes=[B_const2])
    pmark = ar.mark()

    qkm_s = scratch("qkm_s", [8, 128, T], BF16)
    qf_s = scratch("qf_s", [8, 128, T], BF16)
    kf_s = scratch("kf_s", [8, 128, T], BF16)
    ga_s = scratch("ga_s", [8, 128, T], BF16)
    gb_s = scratch("gb_s", [8, 128, T], BF16)
    vm_s = scratch("vm_s", [T, D], BF16)
    om_s = scratch("om_s", [T, D], BF16)
    vf_s = scratch("vf_s", [T, 16, 128], BF16)
    gm_s = scratch("gm_s", [2, 4, T], F32)
    gf_s = scratch("gf_s", [16, T], F32)

    w_mix_v = w_mix.rearrange("(c p) n -> p c n", p=128)

    def mix_phase_fm():
        NT = 256
        m0 = ar.mark()
        wfm = ar.tile([128, 8, 5120], BF16, "wfm")
        wg = ar.tile([128, 8, 24], F32, "wg")
        B_w = [Buf() for _ in range(5)]
        B_wg = Buf()
        segs = [(0, 0, 1024), (1024, O_QF, 1024), (2048, O_KF, 1024), (3072, O_GA, 1024), (4096, O_GB, 1024)]
        for i, (d0, s0, n) in enumerate(segs):
            sc.dma("pool", lambda e, d0=d0, s0=s0, n=n: e.dma_start(out=wfm[:, :, d0:d0 + n], in_=w_mix_v[:, :, s0:s0 + n]),
                   writes=[B_w[i]])
        sc.dma("sp", lambda e: e.dma_start(out=wg[:, :, 0:8], in_=w_mix_v[:, :, O_IM:O_IM + 8]), writes=[B_wg])
        sc.dma("sp", lambda e: e.dma_start(out=wg[:, :, 8:24], in_=w_mix_v[:, :, O_FF:O_FF + 16]), writes=[B_wg])
        gbias = ar.tile([16, 3], F32, "gbias")
        B_gb = Buf()
        sc.dma("sp", lambda e: e.dma_start(out=gbias[0:4, 0:1], in_=b_mix[O_IM:O_IM + 4].rearrange("(p o) -> p o", o=1)), writes=[B_gb])
        sc.dma("sp", lambda e: e.dma_start(out=gbias[0:4, 1:2], in_=b_mix[O_FM:O_FM + 4].rearrange("(p o) -> p o", o=1)), writes=[B_gb])
        sc.dma("sp", lambda e: e.dma_start(out=gbias[0:16, 2:3], in_=b_mix[O_FF:O_FF + 16].rearrange("(p o) -> p o", o=1)), writes=[B_gb])
        xT = [ar.tile([128, 8, NT], F32, "mxT%d" % i) for i in range(2)]
        sq = ar.tile([128, 8, NT], F32, "msq")
        uT = [ar.tile([128, 8, NT], BF16, "muT%d" % i) for i in range(2)]
        stt = [ar.tile([128, NT], F32, "mstt%d" % i) for i in range(2)]
        rstd = [ar.tile([128, NT], F32, "mrstd%d" % i) for i in range(2)]
        R = [ar.tile([128, 8, NT + 3], F32, "R%d" % i) for i in range(2)]
        acc = [ar.tile([128, NT], F32, "acc%d" % i) for i in range(2)]
        tq = [ar.tile([128, 2, NT], F32, "tq%d" % i) for i in range(2)]
        tsq = [ar.tile([128, 2, NT], F32, "tsq%d" % i) for i in range(2)]
        tsd = [ar.tile([128, 2, NT], F32, "tsd%d" % i) for i in range(2)]
        st = {n: [ar.tile([128, 8, NT], BF16, "st_%s%d" % (n, i)) for i in range(2)] for n in ("qkm", "qf", "kf", "ga", "gb")}
        stg = [ar.tile([16, 3, NT], F32, "stg%d" % i) for i in range(2)]
        B_xT = [Buf(), Buf()]; B_sq = Buf(); B_uT = [Buf(), Buf()]; B_stt = [Buf(), Buf()]; B_rstd = [Buf(), Buf()]
        B_R = [Buf(), Buf()]; B_acc = [Buf(), Buf()]; B_tq = [Buf(), Buf()]; B_tsq = [Buf(), Buf()]; B_tsd = [Buf(), Buf()]
        B_st = {n: [Buf(), Buf()] for n in st}
        B_stg = [Buf(), Buf()]
        dsts = {"qkm": qkm_s, "qf": qf_s, "kf": kf_s, "ga": ga_s, "gb": gb_s}
        kscale = 128.0 ** -0.5
        cnt = 0
        def pre_a(it):
            p = it % 2
            b = (it * NT) // S
            t0 = it * NT
            xt = xT[p]
            sc.dma("sp", lambda e, xt=xt, t0=t0: e.dma_start(
                out=xt[:], in_=x1s[:, :, t0:t0 + NT].rearrange("c p n -> p c n")), writes=[B_xT[p]])
            norm_mod(1, xt, B_xT[p], sq, B_sq, stt[p], B_stt[p], rstd[p], B_rstd[p], uT[p], B_uT[p], b, NT, fp32_u=True)

        def pre_b(it):
            p = it % 2
            t0 = it * NT
            for gi, (c0, n, bk, col) in enumerate(((0, 4, 7, 0), (4, 4, 7, NT), (8, 16, 4, NT))):
                for c in range(8):
                    sc.emit("pe", lambda e, c=c, c0=c0, n=n, bk=bk, col=col: e.matmul(
                        banks[bk][0:n, col:col + NT], lhsT=wg[:, c, c0:c0 + n], rhs=sq[:, c, :], start=(c == 0), stop=(c == 7)),
                        reads=[B_wg, B_sq], writes=[bbuf[bk]], inc=(c == 7))
                sc.emit("dve", lambda e, gi=gi, n=n, bk=bk, col=col, p=p: e.tensor_scalar(
                    out=stg[p][0:n, gi, :], in0=banks[bk][0:n, col:col + NT], scalar1=gbias[0:n, gi:gi + 1], scalar2=None, op0=ALU.add),
                    reads=[bbuf[bk], B_gb], writes=[B_stg[p]])
            sc.dma("pool", lambda e, p=p, t0=t0: e.dma_start(out=gm_s[:, :, t0:t0 + NT].rearrange("g h n -> h g n"), in_=stg[p][0:4, 0:2, :]),
                   reads=[B_stg[p]])
            sc.dma("pool", lambda e, p=p, t0=t0: e.dma_start(out=gf_s[:, t0:t0 + NT], in_=stg[p][0:16, 2, :]), reads=[B_stg[p]])

        B_Rc = [[Buf() for _ in range(8)] for _ in range(2)]
        NTL_ = T // NT

        def main(it):
            nonlocal cnt
            p = it % 2
            b = (it * NT) // S
            t0 = it * NT
            ut = uT[p]
            Rp, Rq = R[p], R[1 - p]

            def proj(wcol0, bk, half):
                wseg = wcol0 // 1024
                for c in range(8):
                    sc.emit("pe", lambda e, c=c: e.matmul(
                        banks[bk][:, half * NT:(half + 1) * NT], lhsT=wfm[:, c, wcol0:wcol0 + 128], rhs=ut[:, c, :],
                        start=(c == 0), stop=(c == 7)),
                        reads=[B_w[wseg], B_uT[p]], writes=[bbuf[bk]], inc=(c == 7))

            def conv_group(c):
                def mk(bk):
                    def pe_fn():
                        proj(c * 128, bk, 0)

                    def st_a():
                        if (t0 % S) == 0:
                            sc.emit("pool", lambda e: e.memset(Rp[:, c, 0:3], 0.0), writes=[B_Rc[p][c]])
                        else:
                            sc.emit("pool", lambda e: e.tensor_copy(out=Rp[:, c, 0:3], in_=Rq[:, c, NT:NT + 3]),
                                    reads=[B_Rc[1 - p][c]], writes=[B_Rc[p][c]])
                        sc.emit("act", lambda e: e.activation(
                            out=Rp[:, c, 3:3 + NT], in_=banks[bk][:, 0:NT], func=AF.Identity, bias=cols2[:, 8 + c:9 + c]),
                            reads=[bbuf[bk], B_cols], writes=[B_Rc[p][c]])

                    def st_b():
                        nonlocal cnt
                        a_ = acc[cnt % 2]; Ba = B_acc[cnt % 2]; cnt += 1
                        sc.emit("dve", lambda e: e.tensor_scalar(
                            out=a_[:], in0=Rp[:, c, 0:NT], scalar1=cols1[:, 96 + c:97 + c], scalar2=cols2[:, c:c + 1], op0=ALU.mult, op1=ALU.add),
                            reads=[B_Rc[p][c], B_cols], writes=[Ba])
                        for j in (1, 2, 3):
                            sc.emit("dve", lambda e, j=j: e.scalar_tensor_tensor(
                                out=a_[:], in0=Rp[:, c, j:j + NT], scalar=cols1[:, 96 + 8 * j + c:97 + 8 * j + c], in1=a_[:], op0=ALU.mult, op1=ALU.add),
                                reads=[B_Rc[p][c], B_cols, Ba], writes=[Ba])
                        if c < 4:
                            sc.emit("act", lambda e: e.activation(out=st["qkm"][p][:, c, :], in_=a_[:], func=AF.Silu),
                                    reads=[Ba], writes=[B_st["qkm"][p]])
                        else:
                            sc.emit("act", lambda e: e.activation(out=a_[:], in_=a_[:], func=AF.Silu), reads=[Ba], writes=[Ba])
                            sc.emit("dve", lambda e: e.tensor_scalar(
                                out=st["qkm"][p][:, c, :], in0=a_[:], scalar1=kscale, scalar2=None, op0=ALU.mult),
                                reads=[Ba], writes=[B_st["qkm"][p]])
                    return (pe_fn, st_a, st_b)
                return mk

            def qknorm_group(name, wc0, bcol, gcol, c2):
                def mk(bk):
                    nonlocal cnt
                    i2 = cnt % 2; cnt += 1
                    bb = 5 + i2

                    def pe_fn():
                        for cc in range(2):
                            proj(wc0 + (2 * c2 + cc) * 128, bk, cc)

                    def st_a():
                        pv = banks[bk][:, :].rearrange("p (c n) -> p c n", c=2)
                        sc.emit("dve", lambda e: e.tensor_tensor(
                            out=tq[i2][:], in0=pv, in1=cols2[:, bcol + 2 * c2:bcol + 2 * c2 + 2].unsqueeze(2).to_broadcast([128, 2, NT]), op=ALU.add),
                            reads=[bbuf[bk], B_cols], writes=[B_tq[i2]])
                        sc.emit("act", lambda e: e.activation(out=tsq[i2][:].rearrange("p c n -> p (c n)"),
                                                              in_=tq[i2][:].rearrange("p c n -> p (c n)"), func=AF.Square),
                                reads=[B_tq[i2]], writes=[B_tsq[i2]])
                        sc.emit("pe", lambda e: e.matmul(banks[bb][:, :], lhsT=bones[:], rhs=tsq[i2][:].rearrange("p c n -> p (c n)"),
                                                         start=True, stop=True),
                                reads=[B_tsq[i2], B_const], writes=[bbuf[bb]])

                    def st_b():
                        sc.emit("act", lambda e: e.activation(out=tsd[i2][:].rearrange("p c n -> p (c n)"), in_=banks[bb][:, :],
                                                              func=AF.Ln, scale=1.0 / 64, bias=epsb[:]),
                                reads=[bbuf[bb], B_const2], writes=[B_tsd[i2]])
                        sc.emit("act", lambda e: e.activation(out=tsd[i2][:].rearrange("p c n -> p (c n)"),
                                                              in_=tsd[i2][:].rearrange("p c n -> p (c n)"), func=AF.Exp, scale=-0.5),
                                reads=[B_tsd[i2]], writes=[B_tsd[i2]])
                        sc.emit("dve", lambda e: e.tensor_tensor(out=tq[i2][:], in0=tq[i2][:], in1=tsd[i2][:], op=ALU.mult),
                                reads=[B_tq[i2], B_tsd[i2]], writes=[B_tq[i2]])
                        sc.emit("dve", lambda e: e.tensor_tensor(
                            out=st[name][p][:, 2 * c2:2 * c2 + 2, :], in0=tq[i2][:],
                            in1=cols2[:, gcol + 2 * c2:gcol + 2 * c2 + 2].unsqueeze(2).to_broadcast([128, 2, NT]), op=ALU.mult),
                            reads=[B_tq[i2], B_cols], writes=[B_st[name][p]])
                    return (pe_fn, st_a, st_b)
                return mk

            def sig_group(name, wc0, bcol, c2):
                def mk(bk):
                    def pe_fn():
                        for cc in range(2):
                            proj(wc0 + (2 * c2 + cc) * 128, bk, cc)

                    def st_a():
                        for cc in range(2):
                            c = 2 * c2 + cc
                            sc.emit("act", lambda e, c=c, cc=cc: e.activation(
                                out=st[name][p][:, c, :], in_=banks[bk][:, cc * NT:(cc + 1) * NT], func=AF.Sigmoid, bias=cols2[:, bcol + c:bcol + c + 1]),
                                reads=[bbuf[bk], B_cols], writes=[B_st[name][p]])
                    return (pe_fn, st_a, None)
                return mk

            convs = [conv_group(c) for c in range(8)]
            qkn = [qknorm_group(name, wc0, bcol, gcol, c2) for (name, wc0, bcol, gcol) in (("qf", 1024, 32, 72), ("kf", 2048, 40, 80)) for c2 in range(4)]
            sigs = [sig_group(name, wc0, bcol, c2) for (name, wc0, bcol) in (("ga", 3072, 56), ("gb", 4096, 64)) for c2 in range(4)]
            makers = []
            for i in range(8):
                makers += [qkn[i], convs[i], sigs[i]]
            groups = [mk(gidx % 4) for gidx, mk in enumerate(makers)]
            LA, LB = 2, 4
            n = len(groups)
            for g in range(n + LB):
                if g < n:
                    groups[g][0]()
                if LA <= g < n + LA:
                    groups[g - LA][1]()
                if g >= LB and groups[g - LB][2] is not None:
                    groups[g - LB][2]()
                if it + 1 < NTL_ and MIDPRE:
                    if g == 6:
                        pre_a(it + 1)
                    if g == 15:
                        pre_b(it + 1)
            for name in st:
                sc.dma("pool", lambda e, name=name, p=p, t0=t0: e.dma_start(
                    out=dsts[name][:, :, t0:t0 + NT].rearrange("c p n -> p c n"), in_=st[name][p][:]), reads=[B_st[name][p]])

        pre_a(0)
        pre_b(0)
        for it in range(NTL_):
            if it + 1 < NTL_ and not MIDPRE:
                pre_a(it + 1)
                pre_b(it + 1)
            main(it)
        sc.barrier()
        ar.reset(m0)

    def mix_phase_tm():
        NT = 256
        m0 = ar.mark()
        wtm = ar.tile([128, 8, 3072], BF16, "wtm")
        B_w = [Buf() for _ in range(3)]
        for i, (d0, s0) in enumerate(((0, O_VM), (1024, O_OM), (2048, O_VF))):
            sc.dma("pool", lambda e, d0=d0, s0=s0: e.dma_start(out=wtm[:, :, d0:d0 + 1024], in_=w_mix_v[:, :, s0:s0 + 1024]),
                   writes=[B_w[i]])
        brow = ar.tile([128, 3072], F32, "brow")
        B_brow = Buf()
        for i, s0 in enumerate((O_VM, O_OM, O_VF)):
            sc.dma("sp", lambda e, i=i, s0=s0: e.dma_start(out=brow[:, i * 1024:(i + 1) * 1024], in_=b_mix[s0:s0 + 1024].partition_broadcast(128)),
                   writes=[B_brow])
        xT = [ar.tile([128, 8, NT], F32, "cxT%d" % i) for i in range(2)]
        sq = ar.tile([128, 8, NT], F32, "csq")
        uT = [ar.tile([128, 8, NT], BF16, "cuT%d" % i) for i in range(2)]
        stt = [ar.tile([128, NT], F32, "cstt%d" % i) for i in range(2)]
        rstd = [ar.tile([128, NT], F32, "crstd%d" % i) for i in range(2)]
        svm = [ar.tile([128, 2, D], BF16, "svm%d" % i) for i in range(2)]
        som = [ar.tile([128, 2, D], BF16, "som%d" % i) for i in range(2)]
        svf = [ar.tile([128, 2, 16, 128], BF16, "svf%d" % i) for i in range(2)]
        tmp = [ar.tile([128, 512], F32, "ctmp%d" % i) for i in range(2)]
        B_xT = [Buf(), Buf()]; B_sq = Buf(); B_uT = [Buf(), Buf()]; B_stt = [Buf(), Buf()]; B_rstd = [Buf(), Buf()]
        B_svm = [Buf(), Buf()]; B_som = [Buf(), Buf()]; B_svf = [Buf(), Buf()]; B_tmp = [Buf(), Buf()]
        for i in range(2):
            sc.emit("pool", lambda e, i=i: e.memset(svf[i][:].rearrange("p k h d -> p (k h d)"), 1.0), writes=[B_svf[i]])
        cnt = 0
        def pre(it):
            nonlocal cnt
            p = it % 2
            b = (it * NT) // S
            t0 = it * NT
            xt = xT[p]
            sc.dma("sp", lambda e, xt=xt, t0=t0: e.dma_start(
                out=xt[:], in_=x1s[:, :, t0:t0 + NT].rearrange("c p n -> p c n")), writes=[B_xT[p]])
            norm_mod(1, xt, B_xT[p], sq, B_sq, stt[p], B_stt[p], rstd[p], B_rstd[p], uT[p], B_uT[p], b, NT)
            ut = uT[p]
        def main(it):
            nonlocal cnt
            p = it % 2
            b = (it * NT) // S
            t0 = it * NT
            xt = xT[p]
            ut = uT[p]
            for kb in range(2):
                for grp in range(6):
                    bk = 2 + (cnt % 4); cnt += 1
                    for c in range(8):
                        sc.emit("pe", lambda e, c=c, bk=bk, grp=grp, kb=kb, ut=ut: e.matmul(
                            banks[bk][:, :], lhsT=ut[:, c, kb * 128:(kb + 1) * 128], rhs=wtm[:, c, grp * 512:(grp + 1) * 512],
                            start=(c == 0), stop=(c == 7)),
                            reads=[B_w[grp // 2], B_uT[p]], writes=[bbuf[bk]], inc=(c == 7))
                    br = brow[:, grp * 512:(grp + 1) * 512]
                    if grp < 2:
                        sc.emit("dve", lambda e, bk=bk, grp=grp, kb=kb, p=p, br=br: e.tensor_tensor(
                            out=svm[p][:, kb, grp * 512:(grp + 1) * 512], in0=banks[bk][:, :], in1=br, op=ALU.add),
                            reads=[bbuf[bk], B_brow], writes=[B_svm[p]])
                    elif grp < 4:
                        i2 = cnt % 2
                        sc.emit("dve", lambda e, bk=bk, i2=i2, br=br: e.tensor_tensor(out=tmp[i2][:], in0=banks[bk][:, :], in1=br, op=ALU.add),
                                reads=[bbuf[bk], B_brow], writes=[B_tmp[i2]])
                        sc.emit("act", lambda e, i2=i2, grp=grp, kb=kb, p=p: e.activation(
                            out=som[p][:, kb, (grp - 2) * 512:(grp - 1) * 512], in_=tmp[i2][:], func=AF.Sigmoid),
                            reads=[B_tmp[i2]], writes=[B_som[p]])
                    else:
                        h0 = (grp - 4) * 4
                        svv = svf[p][:].rearrange("p k (hp two) d -> p k hp two d", two=2)
                        pv4 = banks[bk][:, :].rearrange("p (hp two d) -> p hp two d", two=2, d=64)
                        br4 = br.rearrange("p (hp two d) -> p hp two d", two=2, d=64)
                        for hh in range(2):
                            sc.emit("dve", lambda e, kb=kb, h0=h0, hh=hh, svv=svv, pv4=pv4, br4=br4: e.tensor_tensor(
                                out=svv[:, kb, h0:h0 + 4, hh, 64 * hh:64 * hh + 64], in0=pv4[:, :, hh, :],
                                in1=br4[:, :, hh, :], op=ALU.add),
                                reads=[bbuf[bk], B_brow], writes=[B_svf[p]])
            sc.dma("pool", lambda e, p=p, t0=t0: e.dma_start(out=vm_s[t0:t0 + NT, :].rearrange("(k p) d -> p k d", p=128), in_=svm[p][:]),
                   reads=[B_svm[p]])
            sc.dma("pool", lambda e, p=p, t0=t0: e.dma_start(out=om_s[t0:t0 + NT, :].rearrange("(k p) d -> p k d", p=128), in_=som[p][:]),
                   reads=[B_som[p]])
            sc.dma("pool", lambda e, p=p, t0=t0: e.dma_start(out=vf_s[t0:t0 + NT, :, :].rearrange("(k p) h d -> p k h d", p=128), in_=svf[p][:]),
                   reads=[B_svf[p]])
        NTL_ = T // NT
        pre(0)
        for it in range(NTL_):
            if it + 1 < NTL_:
                pre(it + 1)
            main(it)
        sc.barrier()
        ar.reset(m0)

    ya_s = scratch("ya_s", [8, 128, T], BF16)
    yb_s = scratch("yb_s", [8, 128, T], BF16)
    cst2 = din("consts2", [16, 2, S])

    identb = ar.tile([128, 128], BF16, "identb")
    maskb = ar.tile([128, 128], BF16, "maskb")
    B_cb = Buf()
    sc.emit("pool", lambda e: e.tensor_copy(out=identb[:], in_=ident[:]), reads=[B_const], writes=[B_cb])
    sc.emit("pool", lambda e: e.tensor_copy(out=maskb[:], in_=maskadd[:]), reads=[B_const], writes=[B_cb])

    def logsigmoid(n, f_, lf_, t1, t2, Bf, Blf, Bt1, Bt2):
        sc.emit("act", lambda e: e.activation(out=t1[0:n, :], in_=f_[0:n, :], func=AF.Abs), reads=[Bf], writes=[Bt1])
        sc.emit("act", lambda e: e.activation(out=t1[0:n, :], in_=t1[0:n, :], func=AF.Exp, scale=-1.0), reads=[Bt1], writes=[Bt1])
        sc.emit("act", lambda e: e.activation(out=t1[0:n, :], in_=t1[0:n, :], func=AF.Ln, bias=ones[0:n, 0:1]), reads=[Bt1, B_const], writes=[Bt1])
        sc.emit("dve", lambda e: e.tensor_single_scalar(out=t2[0:n, :], in_=f_[0:n, :], scalar=0.0, op=ALU.min), reads=[Bf], writes=[Bt2])
        sc.emit("dve", lambda e: e.tensor_tensor(out=lf_[0:n, :], in0=t2[0:n, :], in1=t1[0:n, :], op=ALU.subtract), reads=[Bt1, Bt2], writes=[Blf])

    def fox_phase():
        m0 = ar.mark()
        onesr = ar.tile([16, S], F32, "onesr")
        B_or = Buf()
        sc.emit("pool", lambda e: e.memset(onesr[:], 1.0), writes=[B_or])
        fg = ar.tile([16, S], F32, "fg"); lf = ar.tile([16, S], F32, "lf"); t1 = ar.tile([16, S], F32, "t1"); t2 = ar.tile([16, S], F32, "t2")
        Fc = ar.tile([16, S], F32, "Fc")
        P3 = ar.tile([16, 3, S], BF16, "P3"); N3 = ar.tile([16, 3, S], BF16, "N3")
        B_fg = Buf(); B_lf = Buf(); B_t1 = Buf(); B_t2 = Buf(); B_F = Buf(); B_P3 = Buf(); B_N3 = Buf()
        onesb = ar.tile([3, S], BF16, "onesb"); B_onesb = Buf()
        sc.emit("pool", lambda e: e.memset(onesb[:], 1.0), writes=[B_onesb])
        qz = [[ar.tile([128, S], BF16, "qz%d%d" % (i, h)) for h in range(2)] for i in range(2)]
        kz = [[ar.tile([128, S], BF16, "kz%d%d" % (i, h)) for h in range(2)] for i in range(2)]
        B_qz = [[Buf(), Buf()], [Buf(), Buf()]]; B_kz = [[Buf(), Buf()], [Buf(), Buf()]]
        for i in range(2):
            for h in range(2):
                sc.emit("pool", lambda e, i=i, h=h: e.memset(qz[i][h][64:128, :], 0.0), writes=[B_qz[i][h]])
                sc.emit("pool", lambda e, i=i, h=h: e.memset(kz[i][h][64:128, :], 0.0), writes=[B_kz[i][h]])
                sc.dma("sp", lambda e, i=i, h=h: e.dma_start(out=qz[i][h][67:70, :], in_=onesb[:]), reads=[B_onesb], writes=[B_qz[i][h]])
                sc.dma("sp", lambda e, i=i, h=h: e.dma_start(out=kz[i][h][64:67, :], in_=onesb[:]), reads=[B_onesb], writes=[B_kz[i][h]])
        vx = [ar.tile([128, 16, 2, 128], BF16, "fvx%d" % i) for i in range(2)]
        yb = [ar.tile([128, S], BF16, "fyb%d" % i) for i in range(2)]
        B_v = [Buf(), Buf()]; B_yb = [Buf(), Buf()]
        NP = 4
        Pt = [ar.tile([128, 512], BF16, "Pt%d" % i) for i in range(NP)]
        B_Pt = [Buf() for _ in range(NP)]
        recT = [ar.tile([128, 512], F32, "recT%d" % i) for i in range(2)]
        rec2 = [ar.tile([128, 512], F32, "rec2%d" % i) for i in range(2)]
        B_recT = [Buf(), Buf()]; B_rec2 = [Buf(), Buf()]
        pi = 0; si = 0; oi = 0; ri = 0; it = 0
        work = []
        chunk_first = []
        chunk_loads = []
        for b in range(BL):
            tb0 = b * S
            def prep(tb0=tb0):
                sc.dma("sp", lambda e, tb0=tb0: e.dma_start(out=fg[:], in_=gf_s[:, tb0:tb0 + S]), writes=[B_fg])
                logsigmoid(16, fg, lf, t1, t2, B_fg, B_lf, B_t1, B_t2)
                sc.emit("dve", lambda e: e.tensor_tensor_scan(out=Fc[:], data0=onesr[:], data1=lf[:], initial=0.0, op0=ALU.mult, op1=ALU.add),
                        reads=[B_or, B_lf], writes=[B_F])
                sc.emit("dve", lambda e: e.tensor_scalar(out=P3[:, 0, :], in0=Fc[:], scalar1=8.0, scalar2=None, op0=ALU.mult), reads=[B_F], writes=[B_P3])
                sc.emit("dve", lambda e: e.scalar_tensor_tensor(out=t1[:], in0=Fc[:], scalar=8.0, in1=P3[:, 0, :], op0=ALU.mult, op1=ALU.subtract),
                        reads=[B_F, B_P3], writes=[B_t1])
                sc.emit("dve", lambda e: e.tensor_copy(out=P3[:, 1, :], in_=t1[:]), reads=[B_t1], writes=[B_P3])
                sc.emit("dve", lambda e: e.tensor_tensor(out=t2[:], in0=t1[:], in1=P3[:, 1, :], op=ALU.subtract), reads=[B_t1, B_P3], writes=[B_t2])
                sc.emit("dve", lambda e: e.tensor_copy(out=P3[:, 2, :], in_=t2[:]), reads=[B_t2], writes=[B_P3])
                sc.emit("dve", lambda e: e.tensor_scalar(out=N3[:].rearrange("p j n -> p (j n)"), in0=P3[:].rearrange("p j n -> p (j n)"),
                                                         scalar1=-1.0, scalar2=None, op0=ALU.mult), reads=[B_P3], writes=[B_N3])
            for c in range(8):
                p = it % 2; it += 1

                def loads(p=p, c=c, tb0=tb0, prep=prep):
                    if c == 0:
                        prep()
                    for hh in range(2):
                        sc.dma("sp", lambda e, hh=hh: e.dma_start(out=qz[p][hh][0:64, :], in_=qf_s[c, 64 * hh:64 * hh + 64, tb0:tb0 + S]),
                               writes=[B_qz[p][hh]])
                        sc.dma("sp", lambda e, hh=hh: e.dma_start(out=kz[p][hh][0:64, :], in_=kf_s[c, 64 * hh:64 * hh + 64, tb0:tb0 + S]),
                               writes=[B_kz[p][hh]])
                        sc.dma("sp", lambda e, hh=hh: e.dma_start(out=qz[p][hh][64:67, :], in_=P3[2 * c + hh:2 * c + hh + 1, :, :]),
                               reads=[B_P3], writes=[B_qz[p][hh]])
                        sc.dma("sp", lambda e, hh=hh: e.dma_start(out=kz[p][hh][67:70, :], in_=N3[2 * c + hh:2 * c + hh + 1, :, :]),
                               reads=[B_N3], writes=[B_kz[p][hh]])
                    sc.dma("sp", lambda e: e.dma_start(
                        out=vx[p][:], in_=vf_s[tb0:tb0 + S, 2 * c:2 * c + 2, :].rearrange("(k p) h d -> p k h d", p=128)), writes=[B_v[p]])
                chunk_first.append(len(work)); chunk_loads.append(loads)
                for hh in range(2):
                    pb = 64 * hh
                    for tb in range(4):
                        t0 = 512 * tb
                        bo = 4 + (oi % 2); oi += 1
                        nsb = 4 * (tb + 1)
                        for sb in range(nsb):
                            s0 = 128 * sb
                            diag = s0 >= t0
                            col0 = (s0 - t0) if diag else 0
                            bs = si % 4; si += 1
                            pt = Pt[pi % NP]; Bp = B_Pt[pi % NP]; pi += 1

                            def front(bs=bs, col0=col0, p=p, pb=pb, s0=s0, t0=t0, hh=hh, diag=diag, pt=pt, Bp=Bp):
                                sc.emit("pe", lambda e: e.matmul(
                                    banks[bs][:, col0:512], lhsT=kz[p][hh][:, s0:s0 + 128], rhs=qz[p][hh][:, t0 + col0:t0 + 512],
                                    start=True, stop=(not diag)), reads=[B_kz[p][hh], B_qz[p][hh]], writes=[bbuf[bs]], inc=(not diag))
                                if diag:
                                    sc.emit("pe", lambda e: e.matmul(
                                        banks[bs][:, col0:col0 + 128], lhsT=identb[:], rhs=maskb[:], start=False, stop=True),
                                        reads=[B_cb], writes=[bbuf[bs]])
                                sc.emit("act", lambda e: e.activation(
                                    out=pt[:, col0:512], in_=banks[bs][:, col0:512], func=AF.Exp, scale=0.125),
                                    reads=[bbuf[bs]], writes=[Bp])

                            last_of_group = (sb == nsb - 1)
                            last_of_chunk = last_of_group and hh == 1 and tb == 3
                            if last_of_group:
                                r = ri % 2; ri += 1
                            else:
                                r = None

                            def back(bo=bo, col0=col0, p=p, sb=sb, hh=hh, pt=pt, Bp=Bp, nsb=nsb, r=r, t0=t0, c=c, tb0=tb0,
                                     last_of_group=last_of_group, last_of_chunk=last_of_chunk):
                                sc.emit("pe", lambda e: e.matmul(
                                    banks[bo][:, col0:512], lhsT=vx[p][:, sb, hh, :], rhs=pt[:, col0:512],
                                    start=(sb == 0), stop=(sb == nsb - 1)), reads=[B_v[p], Bp], writes=[bbuf[bo]], inc=(sb == nsb - 1))
                                if last_of_group:
                                    po, pd = (0, 64) if hh == 0 else (64, 0)
                                    sc.emit("dve", lambda e: e.reciprocal(out=recT[r][pd:pd + 64, :], in_=banks[bo][pd:pd + 64, :]),
                                            reads=[bbuf[bo]], writes=[B_recT[r]])
                                    sc.emit("pool", lambda e: e.tensor_copy(out=rec2[r][po:po + 64, :], in_=recT[r][pd:pd + 64, :]),
                                            reads=[B_recT[r]], writes=[B_rec2[r]])
                                    sc.emit("dve", lambda e: e.tensor_tensor(
                                        out=yb[p][po:po + 64, t0:t0 + 512], in0=banks[bo][po:po + 64, :], in1=rec2[r][po:po + 64, :], op=ALU.mult),
                                        reads=[bbuf[bo], B_rec2[r]], writes=[B_yb[p]])
                                if last_of_chunk:
                                    sc.dma("sp", lambda e: e.dma_start(out=yb_s[c, :, tb0:tb0 + S], in_=yb[p][:]), reads=[B_yb[p]])
                            work.append((front, back))
        LA = 2
        ld_at = {0: chunk_loads[0]}
        for k in range(1, len(chunk_loads)):
            ld_at[chunk_first[k - 1] + LA + 2] = chunk_loads[k]
        for i in range(len(work) + LA):
            if i in ld_at:
                ld_at[i]()
            if i < len(work):
                work[i][0]()
            if i >= LA:
                work[i - LA][1]()
        sc.barrier()
        ar.reset(m0)

    def mlstm_phase():
        m0 = ar.mark()
        L = 128
        NCH = S // L
        scn = ar.tile([4, 2, S], F32, "scn")
        B_scn = Buf()
        sc.dma("sp", lambda e: e.dma_start(out=scn[:], in_=cst2[0:4, :, :]), writes=[B_scn])
        alias = {"ig": 0, "wk": 0, "fg": 1, "lf": 1, "t1": 2, "wi": 2, "t2": 3, "mt": 3, "fl": 3, "bb": 4, "aa": 5, "cm": 6, "rw": 6}
        gt_ = [ar.tile([4, S], F32, "g_%d" % i) for i in range(7)]
        gb_ = [Buf() for _ in range(7)]
        G = {n: gt_[i] for n, i in alias.items()}
        BG = {n: gb_[i] for n, i in alias.items()}
        sm = {n: ar.tile([4, NCH], F32, "s_" + n) for n in ("mlw", "maft", "mprev", "d1", "dec")}
        BS = {n: Buf() for n in sm}
        Rdec = ar.tile([4, 4, NCH], F32, "Rdec"); B_Rdec = Buf()
        LTc = [ar.tile([2, 4, L], F32, "LT%d" % i) for i in range(2)]; RTc = [ar.tile([2, 4, L], F32, "RT%d" % i) for i in range(2)]
        B_LTc = [Buf(), Buf()]; B_RTc = [Buf(), Buf()]
        for i in range(2):
            sc.emit("pool", lambda e, i=i: e.memset(LTc[i][:].rearrange("p h n -> p (h n)"), 1.0), writes=[B_LTc[i]])
            sc.emit("pool", lambda e, i=i: e.memset(RTc[i][:].rearrange("p h n -> p (h n)"), 1.0), writes=[B_RTc[i]])
        tcol = ar.tile([128, NCH, 12], F32, "tcol"); B_tcol = Buf()
        dcol = ar.tile([128, 4, NCH], F32, "dcol"); B_dcol = Buf()
        qk = ar.tile([128, 8, S], BF16, "mqk"); B_qk = Buf()
        vx = ar.tile([128, NCH, 4, 257], BF16, "mvx"); B_vx = Buf()
        gml = ar.tile([128, D], F32, "gml"); B_gml = Buf()
        sc.dma("sp", lambda e: e.dma_start(out=gml[:], in_=g_ml[:].partition_broadcast(128)), writes=[B_gml])
        sc.emit("pool", lambda e: e.memset(vx[:, :, :, 256:257], 1.0), writes=[B_vx])
        Cf = [ar.tile([128, 257], F32, "Cf%d" % h) for h in range(4)]
        Cb = [ar.tile([128, 257], BF16, "Cb%d" % h) for h in range(4)]
        B_Cf = [Buf() for _ in range(4)]; B_Cb = [Buf() for _ in range(4)]
        Pm = [ar.tile([128, 128], F32, "Pm%d" % i) for i in range(3)]; B_Pm = [Buf() for _ in range(3)]
        Sm = [ar.tile([128, 128], BF16, "Sm%d" % i) for i in range(3)]; B_Sm = [Buf() for _ in range(3)]
        kt = [ar.tile([128, 128], BF16, "kt%d" % i) for i in range(3)]; B_kt = [Buf() for _ in range(3)]
        vp = [ar.tile([128, 257], BF16, "vp%d" % i) for i in range(3)]; B_vp = [Buf() for _ in range(3)]
        B_s0 = [Buf() for _ in range(3)]; B_s1 = [Buf() for _ in range(3)]; B_s5 = [Buf() for _ in range(3)]
        iS = [ar.tile([128, 257], F32, "iS%d" % i) for i in range(2)]; B_iS = [Buf(), Buf()]
        tot = [ar.tile([128, 4, 257], F32, "tot%d" % i) for i in range(2)]; B_tot = [Buf(), Buf()]
        hn = ar.tile([128, 4, 256], F32, "hn"); B_hn = Buf()
        sqh = ar.tile([128, 4, 256], F32, "sqh"); B_sqh = Buf()
        sm4 = [ar.tile([128, 4], F32, "sm4_%d" % i) for i in range(4)]; B_sm4 = [Buf() for _ in range(4)]
        om = [ar.tile([128, D], BF16, "om%d" % i) for i in range(2)]; B_om = [Buf(), Buf()]
        yat = ar.tile([128, D], BF16, "yat"); B_yat = Buf()
        yaT = [ar.tile([128, 8, L], BF16, "yaT%d" % i) for i in range(2)]; B_yaT = [Buf(), Buf()]
        v3 = lambda t_: t_[:].rearrange("p (c l) -> p c l", l=L)
        ii = 0
        for b in range(BL):
            tb0 = b * S
            sc.dma("sp", lambda e, tb0=tb0: e.dma_start(out=G["ig"][:], in_=gm_s[0, :, tb0:tb0 + S]), writes=[BG["ig"]])
            sc.dma("sp", lambda e, tb0=tb0: e.dma_start(out=G["fg"][:], in_=gm_s[1, :, tb0:tb0 + S]), writes=[BG["fg"]])
            sc.dma("sp", lambda e, tb0=tb0: e.dma_start(out=qk[:], in_=qkm_s[:, :, tb0:tb0 + S].rearrange("c p n -> p c n")), writes=[B_qk])
            for h in range(4):
                sc.dma("sp", lambda e, tb0=tb0, h=h: e.dma_start(
                    out=vx[:, :, h, 0:256], in_=vm_s[tb0:tb0 + S, h * 256:(h + 1) * 256].rearrange("(k p) d -> p k d", p=128)), writes=[B_vx])
            logsigmoid(4, G["fg"], G["lf"], G["t1"], G["t2"], BG["fg"], BG["lf"], BG["t1"], BG["t2"])
            sc.emit("dve", lambda e: e.tensor_tensor_scan(out=G["bb"][:], data0=scn[0:4, 0, :], data1=G["lf"][:], initial=0.0, op0=ALU.mult, op1=ALU.add),
                    reads=[B_scn, BG["lf"]], writes=[BG["bb"]])
            sc.emit("dve", lambda e: e.tensor_tensor(out=G["aa"][:], in0=G["ig"][:], in1=G["bb"][:], op=ALU.subtract),
                    reads=[BG["ig"], BG["bb"]], writes=[BG["aa"]])
            sc.emit("dve", lambda e: e.tensor_tensor_scan(out=G["cm"][:], data0=scn[0:4, 1, :], data1=G["aa"][:], initial=0.0, op0=ALU.add, op1=ALU.max),
                    reads=[B_scn, BG["aa"]], writes=[BG["cm"]])
            blast = G["bb"][:, L - 1:S:L]
            cml = G["cm"][:, L - 1:S:L]
            sc.emit("dve", lambda e: e.tensor_tensor(out=sm["mlw"][:], in0=blast, in1=cml, op=ALU.add), reads=[BG["bb"], BG["cm"]], writes=[BS["mlw"]])
            sc.emit("dve", lambda e: e.tensor_tensor_scan(out=sm["maft"][:], data0=blast, data1=sm["mlw"][:], initial=0.0, op0=ALU.add, op1=ALU.max),
                    reads=[BG["bb"], BS["mlw"]], writes=[BS["maft"]])
            sc.emit("dve", lambda e: e.memset(sm["mprev"][:, 0:1], 0.0), writes=[BS["mprev"]])
            sc.emit("dve", lambda e: e.tensor_copy(out=sm["mprev"][:, 1:NCH], in_=sm["maft"][:, 0:NCH - 1]), reads=[BS["maft"]], writes=[BS["mprev"]])
            sc.emit("dve", lambda e: e.tensor_tensor(out=v3(G["t1"]), in0=v3(G["bb"]), in1=sm["mprev"][:].unsqueeze(2).to_broadcast([4, NCH, L]), op=ALU.add),
                    reads=[BG["bb"], BS["mprev"]], writes=[BG["t1"]])
            sc.emit("dve", lambda e: e.tensor_tensor(out=G["t2"][:], in0=G["bb"][:], in1=G["cm"][:], op=ALU.add), reads=[BG["bb"], BG["cm"]], writes=[BG["t2"]])
            sc.emit("dve", lambda e: e.tensor_tensor(out=G["mt"][:], in0=G["t1"][:], in1=G["t2"][:], op=ALU.max), reads=[BG["t1"], BG["t2"]], writes=[BG["mt"]])
            sc.emit("dve", lambda e: e.tensor_tensor(out=G["rw"][:], in0=G["bb"][:], in1=G["mt"][:], op=ALU.subtract), reads=[BG["bb"], BG["mt"]], writes=[BG["rw"]])
            sc.emit("dve", lambda e: e.tensor_tensor(out=G["wi"][:], in0=G["t1"][:], in1=G["mt"][:], op=ALU.subtract), reads=[BG["t1"], BG["mt"]], writes=[BG["wi"]])
            sc.emit("act", lambda e: e.activation(out=G["wi"][:], in_=G["wi"][:], func=AF.Exp), reads=[BG["wi"]], writes=[BG["wi"]])
            sc.emit("act", lambda e: e.activation(out=G["fl"][:], in_=G["mt"][:], func=AF.Exp, scale=-1.0), reads=[BG["mt"]], writes=[BG["fl"]])
            sc.emit("dve", lambda e: e.tensor_tensor(out=sm["d1"][:], in0=blast, in1=sm["maft"][:], op=ALU.subtract), reads=[BG["bb"], BS["maft"]], writes=[BS["d1"]])
            sc.emit("dve", lambda e: e.tensor_tensor(out=v3(G["wk"]), in0=v3(G["aa"]), in1=sm["d1"][:].unsqueeze(2).to_broadcast([4, NCH, L]), op=ALU.add),
                    reads=[BG["aa"], BS["d1"]], writes=[BG["wk"]])
            sc.emit("act", lambda e: e.activation(out=G["wk"][:], in_=G["wk"][:], func=AF.Exp), reads=[BG["wk"]], writes=[BG["wk"]])
            sc.emit("dve", lambda e: e.tensor_tensor(out=sm["dec"][:], in0=sm["d1"][:], in1=sm["mprev"][:], op=ALU.add), reads=[BS["d1"], BS["mprev"]], writes=[BS["dec"]])
            sc.emit("act", lambda e: e.activation(out=sm["dec"][:], in_=sm["dec"][:], func=AF.Exp), reads=[BS["dec"]], writes=[BS["dec"]])
            sc.emit("dve", lambda e: e.tensor_tensor(out=Rdec[:], in0=sm["dec"][:].unsqueeze(1).to_broadcast([4, 4, NCH]),
                                                     in1=ident[0:4, 0:4].unsqueeze(2).to_broadcast([4, 4, NCH]), op=ALU.mult),
                    reads=[BS["dec"], B_const], writes=[B_Rdec])
            for ch in range(NCH):
                for qi, nm in enumerate(("wk", "wi", "fl")):
                    sc.emit("pe", lambda e, ch=ch, qi=qi, nm=nm: e.matmul(
                        banks[6][:, ch * 12 + qi * 4: ch * 12 + qi * 4 + 4], lhsT=G[nm][0:4, ch * L:(ch + 1) * L], rhs=ident[0:4, 0:4],
                        start=True, stop=True), reads=[BG[nm], B_const], writes=[bbuf[6]], inc=(qi == 2))
            sc.emit("dve", lambda e: e.tensor_copy(out=tcol[:].rearrange("p c q -> p (c q)"), in_=banks[6][:, 0:NCH * 12]), reads=[bbuf[6]], writes=[B_tcol])
            sc.emit("pe", lambda e: e.matmul(banks[6][:, 256:256 + 4 * NCH], lhsT=ones[0:4, :], rhs=Rdec[:].rearrange("k h c -> k (h c)"),
                                             start=True, stop=True), reads=[B_Rdec, B_const], writes=[bbuf[6]])
            sc.emit("dve", lambda e: e.tensor_copy(out=dcol[:].rearrange("p h c -> p (h c)"), in_=banks[6][:, 256:256 + 4 * NCH]), reads=[bbuf[6]], writes=[B_dcol])
            for h in range(4):
                sc.emit("pool", lambda e, h=h: e.memset(Cf[h][:], 0.0), writes=[B_Cf[h]])
                sc.emit("pool", lambda e, h=h: e.memset(Cb[h][:], 0.0), writes=[B_Cb[h]])
            def post(ch, tp, tb0):
                tt = tot[tp]
                sc.emit("act", lambda e, tt=tt: e.activation(out=sm4[0][:], in_=tt[:, :, 256], func=AF.Abs),
                        reads=[B_tot[tp]], writes=[B_sm4[0]])
                sc.emit("dve", lambda e, ch=ch: e.tensor_tensor(out=sm4[0][:], in0=sm4[0][:], in1=tcol[:, ch, 8:12], op=ALU.max),
                        reads=[B_sm4[0], B_tcol], writes=[B_sm4[0]])
                sc.emit("dve", lambda e: e.reciprocal(out=sm4[1][:], in_=sm4[0][:]), reads=[B_sm4[0]], writes=[B_sm4[1]])
                sc.emit("dve", lambda e, tt=tt: e.tensor_tensor(out=hn[:], in0=tt[:, :, 0:256], in1=sm4[1][:].unsqueeze(2).to_broadcast([128, 4, 256]), op=ALU.mult),
                        reads=[B_tot[tp], B_sm4[1]], writes=[B_hn])
                sc.emit("act", lambda e: e.activation(out=sqh[:].rearrange("p h d -> p (h d)"), in_=hn[:].rearrange("p h d -> p (h d)"), func=AF.Square),
                        reads=[B_hn], writes=[B_sqh])
                sc.emit("dve", lambda e: e.tensor_reduce(out=sm4[2][:], in_=sqh[:], axis=AX.X, op=ALU.add), reads=[B_sqh], writes=[B_sm4[2]])
                sc.emit("act", lambda e: e.activation(out=sm4[2][:], in_=sm4[2][:], func=AF.Sqrt, scale=1.0 / 256, bias=epsb[:]),
                        reads=[B_sm4[2], B_const2], writes=[B_sm4[2]])
                sc.emit("dve", lambda e: e.reciprocal(out=sm4[3][:], in_=sm4[2][:]), reads=[B_sm4[2]], writes=[B_sm4[3]])
                sc.emit("dve", lambda e: e.tensor_tensor(out=hn[:], in0=hn[:], in1=sm4[3][:].unsqueeze(2).to_broadcast([128, 4, 256]), op=ALU.mult),
                        reads=[B_hn, B_sm4[3]], writes=[B_hn])
                sc.emit("dve", lambda e: e.tensor_tensor(out=hn[:].rearrange("p h d -> p (h d)"), in0=hn[:].rearrange("p h d -> p (h d)"), in1=gml[:], op=ALU.mult),
                        reads=[B_hn, B_gml], writes=[B_hn])
                sc.emit("dve", lambda e, tp=tp: e.tensor_tensor(out=yat[:], in0=hn[:].rearrange("p h d -> p (h d)"), in1=om[tp][:], op=ALU.mult),
                        reads=[B_hn, B_om[tp]], writes=[B_yat])
                for half in range(2):
                    bk = 6 + half
                    for e4 in range(4):
                        ec = half * 4 + e4
                        sc.emit("pe", lambda e, bk=bk, e4=e4, ec=ec: e.matmul(
                            banks[bk][:, e4 * L:(e4 + 1) * L], lhsT=yat[:, ec * 128:(ec + 1) * 128], rhs=identb[:], start=True, stop=True),
                            reads=[B_yat, B_cb], writes=[bbuf[bk]], inc=(e4 == 3))
                    if half == 0:
                        sc.emit("act", lambda e, bk=bk, tp=tp, half=half: e.copy(
                            out=yaT[tp][:, half * 4:half * 4 + 4, :].rearrange("p c n -> p (c n)"), in_=banks[bk][:, :]),
                            reads=[bbuf[bk]], writes=[B_yaT[tp]])
                    else:
                        sc.emit("dve", lambda e, bk=bk, tp=tp, half=half: e.tensor_copy(
                            out=yaT[tp][:, half * 4:half * 4 + 4, :].rearrange("p c n -> p (c n)"), in_=banks[bk][:, :]),
                            reads=[bbuf[bk]], writes=[B_yaT[tp]])
                sc.dma("sp", lambda e, tp=tp, ch=ch, tb0=tb0: e.dma_start(
                    out=ya_s[:, :, tb0 + ch * L: tb0 + (ch + 1) * L].rearrange("c p n -> p c n"), in_=yaT[tp][:]), reads=[B_yaT[tp]])
            items = []

            def chunk_loads(ch, tb0=tb0):
                tp = ch % 2
                cs = slice(ch * L, (ch + 1) * L)
                sc.dma("sp", lambda e: e.dma_start(out=om[tp][:], in_=om_s[tb0 + ch * L: tb0 + (ch + 1) * L, :]), writes=[B_om[tp]])
                for h in range(4):
                    sc.dma("sp", lambda e, h=h: e.dma_start(out=LTc[tp][0:1, h, :], in_=G["aa"][h:h + 1, cs]), reads=[BG["aa"]], writes=[B_LTc[tp]])
                    sc.dma("sp", lambda e, h=h: e.dma_start(out=RTc[tp][1:2, h, :], in_=G["rw"][h:h + 1, cs]), reads=[BG["rw"]], writes=[B_RTc[tp]])

            chunk_loads(0)
            for ch in range(NCH):
                cs = slice(ch * L, (ch + 1) * L)
                tp = ch % 2
                for h in range(4):
                    i3 = ii % 3; i2 = ii % 2; ii += 1
                    qTc = qk[:, h, cs]; kTc = qk[:, 4 + h, cs]
                    sl = slice(i3 * L, (i3 + 1) * L)

                    def stA(ch=ch, h=h, tp=tp, i3=i3, qTc=qTc, kTc=kTc):
                        if h == 2 and ch + 1 < NCH:
                            chunk_loads(ch + 1)
                        bka = (0, 1, 5)[i3]
                        Ba = bbuf[bka]
                        sc.emit("pe", lambda e: e.matmul(banks[bka][:, 0:L], lhsT=kTc, rhs=qTc, start=True, stop=True),
                                reads=[B_qk], writes=[Ba], inc=False)
                        sc.emit("pe", lambda e: e.matmul(banks[bka][:, L:2 * L], lhsT=LTc[tp][0:2, h, :], rhs=RTc[tp][0:2, h, :], start=True, stop=False),
                                reads=[B_LTc[tp], B_RTc[tp]], writes=[Ba], inc=False)
                        sc.emit("pe", lambda e: e.matmul(banks[bka][:, L:2 * L], lhsT=identb[:], rhs=maskb[:], start=False, stop=True),
                                reads=[B_cb], writes=[Ba], inc=False)
                        sc.emit("pe", lambda e: e.matmul(banks[bka][:, 2 * L:3 * L], lhsT=kTc, rhs=identb[:], start=True, stop=True),
                                reads=[B_qk, B_cb], writes=[Ba])
                        sc.emit("act", lambda e: e.activation(out=Pm[i3][:], in_=banks[bka][:, L:2 * L], func=AF.Exp), reads=[Ba], writes=[B_Pm[i3]])
                        sc.emit("dve", lambda e: e.tensor_tensor(out=Sm[i3][:], in0=banks[bka][:, 0:L], in1=Pm[i3][:], op=ALU.mult),
                                reads=[Ba, B_Pm[i3]], writes=[B_Sm[i3]])
                        sc.emit("act", lambda e: e.copy(out=kt[i3][:], in_=banks[bka][:, 2 * L:3 * L]), reads=[Ba], writes=[B_kt[i3]])
                        sc.emit("pool", lambda e: e.tensor_scalar(
                            out=vp[i3][:], in0=vx[:, ch, h, :], scalar1=tcol[:, ch, h:h + 1], scalar2=None, op0=ALU.mult),
                            reads=[B_vx, B_tcol], writes=[B_vp[i3]])

                    def stB(ch=ch, h=h, tp=tp, i3=i3, i2=i2, qTc=qTc, tb0=tb0):
                        sc.emit("pe", lambda e: e.matmul(banks[4][:, 0:257], lhsT=kt[i3][:], rhs=vp[i3][:], start=True, stop=True),
                                reads=[B_kt[i3], B_vp[i3]], writes=[bbuf[4]])
                        sc.emit("pe", lambda e: e.matmul(banks[3][:, 0:257], lhsT=qTc, rhs=Cb[h][:], start=True, stop=True),
                                reads=[B_qk, B_Cb[h]], writes=[bbuf[3]])
                        sc.emit("pe", lambda e: e.matmul(banks[2][:, 0:257], lhsT=Sm[i3][:], rhs=vx[:, ch, h, :], start=True, stop=True),
                                reads=[B_Sm[i3], B_vx], writes=[bbuf[2]])
                        sc.emit("dve", lambda e: e.scalar_tensor_tensor(
                            out=Cf[h][:], in0=Cf[h][:], scalar=dcol[:, h, ch:ch + 1], in1=banks[4][:, 0:257], op0=ALU.mult, op1=ALU.add),
                            reads=[B_Cf[h], B_dcol, bbuf[4]], writes=[B_Cf[h]])
                        sc.emit("pool", lambda e: e.tensor_copy(out=Cb[h][:], in_=Cf[h][:]), reads=[B_Cf[h]], writes=[B_Cb[h]])
                        sc.emit("act", lambda e: e.activation(out=iS[i2][:], in_=banks[3][:, 0:257], func=AF.Copy, scale=tcol[:, ch, 4 + h:5 + h]),
                                reads=[bbuf[3], B_tcol], writes=[B_iS[i2]])
                        sc.emit("dve", lambda e: e.tensor_tensor(out=tot[tp][:, h, :], in0=banks[2][:, 0:257], in1=iS[i2][:], op=ALU.add),
                                reads=[bbuf[2], B_iS[i2]], writes=[B_tot[tp]])
                        if h == 3:
                            post(ch, tp, tb0)
                    items.append((stA, stB))
            LAG = 2
            for i in range(len(items) + LAG):
                if i < len(items):
                    items[i][0]()
                if i >= LAG:
                    items[i - LAG][1]()
        sc.barrier()
        ar.reset(m0)

    def merge_phase():
        NT = 512
        m0 = ar.mark()
        W = {}
        BW = {}
        for n, d_ in (("a", w_ba), ("b", w_bb), ("o", w_o)):
            W[n] = ar.tile([128, 8, D], BF16, "w_" + n)
            BW[n] = Buf()
            sc.dma("pool", lambda e, n=n, d_=d_: e.dma_start(out=W[n][:], in_=d_.rearrange("(c p) n -> p c n", p=128)), writes=[BW[n]])
        T5 = {}
        B5 = {}
        srcs = {"ya": ya_s, "yb": yb_s, "ga": ga_s, "gb": gb_s}
        for n in srcs:
            T5[n] = [ar.tile([128, 8, NT], BF16, "m_%s%d" % (n, i)) for i in range(2)]
            B5[n] = [Buf(), Buf()]
        xT = [ar.tile([128, 8, NT], F32, "m_x%d" % i) for i in range(2)]; B_x = [Buf(), Buf()]
        mg = ar.tile([128, 8, NT], BF16, "m_mg"); B_mg = [Buf() for _ in range(8)]
        m1 = [ar.tile([128, NT], F32, "m_m1%d" % i) for i in range(2)]; B_m1 = [Buf(), Buf()]
        m2 = [ar.tile([128, NT], F32, "m_m2%d" % i) for i in range(2)]; B_m2 = [Buf(), Buf()]
        k = 0
        for it in range(T // NT):
            p = it % 2
            b = (it * NT) // S
            t0 = it * NT
            for n in srcs:
                sc.dma("sp", lambda e, n=n, p=p, t0=t0: e.dma_start(out=T5[n][p][:], in_=srcs[n][:, :, t0:t0 + NT].rearrange("c p n -> p c n")),
                       writes=[B5[n][p]])
            sc.dma("sp", lambda e, p=p, t0=t0: e.dma_start(out=xT[p][:], in_=x1s[:, :, t0:t0 + NT].rearrange("c p n -> p c n")), writes=[B_x[p]])
            for c in range(8):
                i2 = k % 2; k += 1
                for (bk, wn, yn) in ((0, "a", "ya"), (1, "b", "yb")):
                    bk = bk + 2 * i2
                    for ec in range(8):
                        sc.emit("pe", lambda e, bk=bk, wn=wn, yn=yn, ec=ec, c=c, p=p: e.matmul(
                            banks[bk][:, :], lhsT=W[wn][:, ec, c * 128:(c + 1) * 128], rhs=T5[yn][p][:, ec, :], start=(ec == 0), stop=(ec == 7)),
                            reads=[BW[wn], B5[yn][p]], writes=[bbuf[bk]], inc=(ec == 7))
                sc.emit("dve", lambda e, i2=i2, c=c, p=p: e.tensor_tensor(out=m1[i2][:], in0=banks[2 * i2][:, :], in1=T5["ga"][p][:, c, :], op=ALU.mult),
                        reads=[bbuf[2 * i2], B5["ga"][p]], writes=[B_m1[i2]])
                sc.emit("dve", lambda e, i2=i2, c=c, p=p: e.tensor_tensor(out=m2[i2][:], in0=banks[1 + 2 * i2][:, :], in1=T5["gb"][p][:, c, :], op=ALU.mult),
                        reads=[bbuf[1 + 2 * i2], B5["gb"][p]], writes=[B_m2[i2]])
                sc.emit("pool", lambda e, i2=i2, c=c: e.tensor_tensor(out=mg[:, c, :], in0=m1[i2][:], in1=m2[i2][:], op=ALU.add),
                        reads=[B_m1[i2], B_m2[i2]], writes=[B_mg[c]])
            for c in range(8):
                bk = 4 + (c % 2)
                for ec in range(8):
                    sc.emit("pe", lambda e, bk=bk, ec=ec, c=c: e.matmul(
                        banks[bk][:, :], lhsT=W["o"][:, ec, c * 128:(c + 1) * 128], rhs=mg[:, ec, :], start=(ec == 0), stop=(ec == 7)),
                        reads=[BW["o"], B_mg[ec]], writes=[bbuf[bk]], inc=(ec == 7))
                sc.emit("dve", lambda e, bk=bk, c=c, p=p, b=b: e.scalar_tensor_tensor(
                    out=xT[p][:, c, :], in0=banks[bk][:, :], scalar=coef[:, 5, c, b:b + 1], in1=xT[p][:, c, :], op0=ALU.mult, op1=ALU.add),
                    reads=[bbuf[bk], B_coef, B_x[p]], writes=[B_x[p]])
            sc.dma("pool", lambda e, p=p, t0=t0: e.dma_start(out=x1s[:, :, t0:t0 + NT].rearrange("c p n -> p c n"), in_=xT[p][:]), reads=[B_x[p]])
        sc.barrier()
        ar.reset(m0)

    pmark = ar.mark()
    if debug == ("ffn1only",):
        ffn_phase(0, w1_in, w1_out, x_in, y_out, None, None)
    else:
        ffn_phase(0, w1_in, w1_out, x_in, None, None, x1s)
        mix_phase_fm()
        mix_phase_tm()
        mlstm_phase()
        fox_phase()
        if not (debug and "nomerge" in debug):
            merge_phase()
            ffn_phase(2, w2_in, w2_out, None, y_out, x1s, None)

    for ev in out_evs:
        sc.need("sp", ev)

    dl = sc.check_deadlock()
    assert dl is None, "sync deadlock: %r" % (dl,)
    with nc.Block() as block:
        sc.run(block)
    return nc


def make_consts():
    c = np.zeros((128, 640), np.float32)
    c[:, 0:128] = np.eye(128, dtype=np.float32)
    s = np.arange(128)[:, None]
    t = np.arange(128)[None, :]
    c[:, 128:256] = np.where(s <= t, 0.0, NEG).astype(np.float32)
    c[:, 256:384] = 1.0
    c[:, 384:512] = ((s // 64) == (t // 64)).astype(np.float32)
    c[:, 512:640] = np.eye(128, dtype=np.float32)
    return c


def make_consts2():
    c = np.zeros((16, 2, S), np.float32)
    c[:, 0, :] = 1.0
    c[:, 0, ::128] = 0.0
    c[:, 1, ::128] = NEG
    return c


_NC_CACHE = {}


def kernel(**inputs):
    debug = inputs.pop("_debug", None)
    ncores = inputs.pop("_ncores", NCORES)
    key = debug
    if key not in _NC_CACHE:
        _NC_CACHE[key] = build_program(debug)
    nc = _NC_CACHE[key]
    consts = make_consts()
    f = lambda a: np.ascontiguousarray(np.asarray(a, dtype=np.float32))
    shared = {
        "w_ada": f(inputs["w_ada"][0]), "b_ada": f(inputs["b_ada"][0]),
        "ffn1_norm_g": f(inputs["ffn1_norm_g"][0]), "ffn1_w_in": f(inputs["ffn1_w_in"][0]),
        "ffn1_w_out": f(inputs["ffn1_w_out"][0]), "mix_norm_g": f(inputs["mix_norm_g"][0]),
        "w_mix": f(inputs["w_mix"][0]), "b_mix": f(inputs["b_mix"][0]),
        "conv_w": f(inputs["conv_w"][0]), "conv_b": f(inputs["conv_b"][0]),
        "mlstm_norm_g": f(inputs["mlstm_norm_g"][0]).reshape(-1),
        "fox_q_norm_g": f(inputs["fox_q_norm_g"][0]).reshape(-1),
        "fox_k_norm_g": f(inputs["fox_k_norm_g"][0]).reshape(-1),
        "w_branch_a": f(inputs["w_branch_a"][0]), "w_branch_b": f(inputs["w_branch_b"][0]),
        "w_out": f(inputs["w_out"][0]), "ffn2_norm_g": f(inputs["ffn2_norm_g"][0]),
        "ffn2_w_in": f(inputs["ffn2_w_in"][0]), "ffn2_w_out": f(inputs["ffn2_w_out"][0]),
        "consts": consts,
        "consts2": make_consts2(),
    }
    x = np.asarray(inputs["x"], dtype=np.float32)
    c = np.asarray(inputs["c"], dtype=np.float32)
    in_maps = []
    for i in range(ncores):
        m = dict(shared)
        m["x"] = np.ascontiguousarray(x[i * BL:(i + 1) * BL].reshape(T, D))
        m["c"] = np.ascontiguousarray(c[i * BL:(i + 1) * BL])
        in_maps.append(m)
    res = run_bass_kernel_spmd(nc, in_maps, core_ids=list(range(ncores)))
    if debug:
        return res.results
    out = np.concatenate([r["y"].reshape(BL, S, D) for r in res.results], axis=0)
    return out.astype(np.float32)
```

```python
import numpy as np
import concourse.bass as bass
import concourse.mybir as mybir
from concourse.alu_op_type import AluOpType as ALU
from concourse.bass_utils import run_bass_kernel_spmd

F32 = mybir.dt.float32
BF16 = mybir.dt.bfloat16
AF = mybir.ActivationFunctionType
AX = mybir.AxisListType

D = 1024
S = 2048
BL = 2
T = BL * S
DFF = 2816
NJ = DFF // 128
NMOD = 9
MIXW = 8216
EPS = 1e-6
NCORES = 8
NEG = -1.0e30

O_QM, O_KM, O_VM, O_OM, O_IM, O_FM = 0, 512, 1024, 2048, 3072, 3076
O_QF, O_KF, O_VF, O_FF, O_GA, O_GB = 3080, 4104, 5128, 6152, 6168, 7192


class Buf:
    __slots__ = ("w", "weng", "rs")

    def __init__(self):
        self.w = None
        self.weng = None
        self.rs = []


class Sched:
    def __init__(self, nc):
        self.nc = nc
        self.ops = {k: [] for k in ("pe", "act", "dve", "pool", "sp")}
        self.sems = []
        self.semval = []
        self.esem = {}
        for k in self.ops:
            self.esem[k] = self._newsem("e_" + k)
        self.known = {k: {} for k in self.ops}
        self.dring = {}
        self.dcnt = {}
        for q, n in (("sp", 16), ("pool", 8), ("act", 4)):
            self.dring[q] = [self._newsem("d_%s%d" % (q, i)) for i in range(n)]
            self.dcnt[q] = 0
        self.dlast = {}

    def _newsem(self, name):
        self.sems.append(self.nc.alloc_semaphore(name))
        self.semval.append(0)
        return len(self.sems) - 1

    def need(self, eng, ev):
        if ev is None:
            return
        si, val = ev
        if self.known[eng].get(si, 0) >= val:
            return
        self.known[eng][si] = val
        sem = self.sems[si]
        self.ops[eng].append(lambda e, sem=sem, val=val: e.wait_ge(sem, val))

    def emit(self, eng, fn, reads=(), writes=(), inc=True):
        for b in reads:
            if b.w is not None and not (b.weng == eng and eng == "pe"):
                self.need(eng, b.w)
        for b in writes:
            for (ev, re) in b.rs:
                if re != eng or eng != "pe":
                    self.need(eng, ev)
            if b.w is not None and (b.weng != eng or eng != "pe"):
                self.need(eng, b.w)
        si = self.esem[eng]
        ev = (si, self.semval[si] + 1)
        if inc:
            self.semval[si] += 1
            sem = self.sems[si]
            self.ops[eng].append(lambda e, fn=fn, sem=sem: fn(e).then_inc(sem, 1))
        else:
            self.ops[eng].append(lambda e, fn=fn: fn(e))
        for b in reads:
            b.rs.append((ev, eng))
        for b in writes:
            b.w = ev
            b.weng = eng
            b.rs = []
        return ev

    def dma(self, q, fn, reads=(), writes=()):
        ring = self.dring[q]
        k = self.dcnt[q] % len(ring)
        self.dcnt[q] += 1
        si = ring[k]
        if self.semval[si] > 0:
            self.need(q, (si, self.semval[si]))
        for b in reads:
            if b.w is not None:
                self.need(q, b.w)
        for b in writes:
            for (ev, re) in b.rs:
                self.need(q, ev)
            if b.w is not None:
                self.need(q, b.w)
        self.semval[si] += 16
        ev = (si, self.semval[si])
        sem = self.sems[si]
        self.ops[q].append(lambda e, fn=fn, sem=sem: fn(e).then_inc(sem, 16))
        for b in reads:
            b.rs.append((ev, "dma"))
        for b in writes:
            b.w = ev
            b.weng = "dma"
            b.rs = []
        return ev

    def barrier(self):
        for eng in self.ops:
            for si in range(len(self.sems)):
                if self.semval[si] > 0:
                    self.need(eng, (si, self.semval[si]))

    def run(self, block):
        def mk(name):
            def f(e):
                for op in self.ops[name]:
                    op(e)
            return f
        block.tensor(mk("pe"))
        block.scalar(mk("act"))
        block.vector(mk("dve"))
        block.gpsimd(mk("pool"))
        block.sync(mk("sp"))


class Arena:
    def __init__(self, nc):
        self.nc = nc
        self.base = ((nc.sbuf_base + 63) // 64) * 64
        self.top = nc.sbuf_top
        self.cur = self.base
        self.n = 0

    def mark(self):
        return self.cur

    def reset(self, m):
        self.cur = m

    def tile(self, shape, dtype, name="t"):
        nbytes = int(np.prod(shape[1:])) * (2 if dtype == BF16 else 4)
        nbytes = ((nbytes + 63) // 64) * 64
        off = self.cur
        self.cur += nbytes
        assert self.cur <= self.top, "SBUF overflow %d > %d (%s)" % (self.cur, self.top, name)
        self.n += 1
        return self.nc.alloc_sbuf_tensor_at("%s_%d" % (name, self.n), list(shape), dtype, offset=off)


def build_program(debug=None):
    nc = bass.Bass("TRN2", target_bir_lowering=False)
    dt = {}

    def din(name, shape, dtype=F32):
        dt[name] = nc.dram_tensor(name, list(shape), dtype, kind="ExternalInput").ap()
        return dt[name]

    x_in = din("x", [T, D])
    c_in = din("c", [BL, D])
    w_ada = din("w_ada", [D, NMOD * D])
    b_ada = din("b_ada", [NMOD * D])
    g_f1 = din("ffn1_norm_g", [D])
    w1_in = din("ffn1_w_in", [D, 2 * DFF])
    w1_out = din("ffn1_w_out", [DFF, D])
    g_mix = din("mix_norm_g", [D])
    w_mix = din("w_mix", [D, MIXW])
    b_mix = din("b_mix", [MIXW])
    conv_w = din("conv_w", [4, D])
    conv_b = din("conv_b", [D])
    g_ml = din("mlstm_norm_g", [D])
    g_fq = din("fox_q_norm_g", [D])
    g_fk = din("fox_k_norm_g", [D])
    w_ba = din("w_branch_a", [D, D])
    w_bb = din("w_branch_b", [D, D])
    w_o = din("w_out", [D, D])
    g_f2 = din("ffn2_norm_g", [D])
    w2_in = din("ffn2_w_in", [D, 2 * DFF])
    w2_out = din("ffn2_w_out", [DFF, D])
    cst = din("consts", [128, 640])
    y_out = nc.dram_tensor("y", [T, D], F32, kind="ExternalOutput").ap()

    def scratch(name, shape, dtype):
        kind = "ExternalOutput" if (debug and name in debug) else "Internal"
        return nc.dram_tensor(name, list(shape), dtype, kind=kind).ap()

    x1s = scratch("x1s", [8, 128, T], F32)

    sc = Sched(nc)
    ar = Arena(nc)

    banks = [nc.alloc_psum_tensor("bank%d" % i, [128, 512], F32) for i in range(8)]
    bbuf = [Buf() for _ in range(8)]

    ident = ar.tile([128, 128], F32, "ident")
    maskadd = ar.tile([128, 128], F32, "maskadd")
    ones = ar.tile([128, 128], F32, "ones")
    bones = ar.tile([128, 128], F32, "bones")
    cols1 = ar.tile([128, 128], F32, "cols1")
    cols2 = ar.tile([128, 128], F32, "cols2")
    modT = ar.tile([128, 72, 2], F32, "modT")
    coef = ar.tile([128, 9, 8, 2], F32, "coef")
    B_const = Buf()
    B_cols = Buf()
    B_mod = Buf()
    B_coef = Buf()

    sc.dma("sp", lambda e: e.dma_start(out=ident[:], in_=cst[:, 0:128]), writes=[B_const])
    sc.dma("sp", lambda e: e.dma_start(out=maskadd[:], in_=cst[:, 128:256]), writes=[B_const])
    sc.dma("sp", lambda e: e.dma_start(out=ones[:], in_=cst[:, 256:384]), writes=[B_const])
    sc.dma("sp", lambda e: e.dma_start(out=bones[:], in_=cst[:, 384:512]), writes=[B_const])

    pmark = ar.mark()

    rows1 = ar.tile([128, 128], F32, "rows1")
    rows2 = ar.tile([128, 128], F32, "rows2")
    B_rows = Buf()
    sc.emit("pool", lambda e: e.memset(rows2[:], 0.0), writes=[B_rows])

    def rowdma(dst, r0, nr, src1d, off):
        sc.dma("sp", lambda e: e.dma_start(
            out=dst[r0:r0 + nr, :],
            in_=src1d[off:off + nr * 128].rearrange("(r p) -> r p", p=128)), writes=[B_rows])

    rowdma(rows1, 0, 72, b_ada, 0)
    rowdma(rows1, 72, 8, g_f1, 0)
    rowdma(rows1, 80, 8, g_mix, 0)
    rowdma(rows1, 88, 8, g_f2, 0)
    for j in range(4):
        sc.dma("sp", lambda e, j=j: e.dma_start(
            out=rows1[96 + 8 * j:104 + 8 * j, :],
            in_=conv_w[j, :].rearrange("(r p) -> r p", p=128)), writes=[B_rows])
    rowdma(rows2, 0, 8, conv_b, 0)
    rowdma(rows2, 8, 24, b_mix, 0)
    rowdma(rows2, 32, 24, b_mix, O_QF)
    rowdma(rows2, 56, 16, b_mix, O_GA)
    rowdma(rows2, 72, 8, g_fq, 0)
    rowdma(rows2, 80, 8, g_fk, 0)
    for b in range(BL):
        sc.dma("sp", lambda e, b=b: e.dma_start(
            out=rows2[96 + 8 * b:104 + 8 * b, :],
            in_=c_in[b, :].rearrange("(r p) -> r p", p=128)), writes=[B_rows])

    sc.emit("pe", lambda e: e.transpose(out=banks[0][:, 0:128], in_=rows1[:], identity=ident[:]),
            reads=[B_rows, B_const], writes=[bbuf[0]])
    sc.emit("pe", lambda e: e.transpose(out=banks[0][:, 128:256], in_=rows2[:], identity=ident[:]),
            reads=[B_rows, B_const], writes=[bbuf[0]])
    sc.emit("dve", lambda e: e.tensor_copy(out=cols1[:], in_=banks[0][:, 0:128]), reads=[bbuf[0]], writes=[B_cols])
    sc.emit("dve", lambda e: e.tensor_copy(out=cols2[:], in_=banks[0][:, 128:256]), reads=[bbuf[0]], writes=[B_cols])

    scT = ar.tile([128, BL, 8], BF16, "scT")
    B_scT = Buf()
    sc.emit("act", lambda e: e.activation(out=scT[:].rearrange("p b c -> p (b c)"), in_=cols2[:, 96:112], func=AF.Silu),
            reads=[B_cols], writes=[B_scT])

    WB = 1152
    wa = [ar.tile([128, 8, WB], BF16, "wada%d" % i) for i in range(2)]
    B_wa = [Buf(), Buf()]
    w_ada_v = w_ada.rearrange("(c p) n -> p c n", p=128)
    for blk in range(8):
        t = wa[blk % 2]
        sc.dma("pool", lambda e, t=t, blk=blk: e.dma_start(out=t[:], in_=w_ada_v[:, :, blk * WB:(blk + 1) * WB]),
               writes=[B_wa[blk % 2]])
        for jj in range(9):
            j = blk * 9 + jj
            for cc in range(8):
                sc.emit("pe", lambda e, t=t, jj=jj, cc=cc, j=j: e.matmul(
                    banks[1][:, 2 * j:2 * j + 2], lhsT=t[:, cc, jj * 128:(jj + 1) * 128], rhs=scT[:, :, cc],
                    start=(cc == 0), stop=(cc == 7)),
                    reads=[B_wa[blk % 2], B_scT], writes=[bbuf[1]], inc=(cc == 7))
    pm = banks[1][:, 0:144].rearrange("p (j b) -> p j b", b=2)
    for b in range(BL):
        sc.emit("dve", lambda e, b=b: e.tensor_tensor(out=modT[:, :, b], in0=pm[:, :, b], in1=cols1[:, 0:72], op=ALU.add),
                reads=[bbuf[1], B_cols], writes=[B_mod])
    for k, gcol in enumerate((72, 80, 88)):
        for b in range(BL):
            sh = modT[:, (3 * k) * 8:(3 * k) * 8 + 8, b]
            scl = modT[:, (3 * k + 1) * 8:(3 * k + 1) * 8 + 8, b]
            gg = modT[:, (3 * k + 2) * 8:(3 * k + 2) * 8 + 8, b]
            sc.emit("dve", lambda e, k=k, b=b, scl=scl, gcol=gcol: e.scalar_tensor_tensor(
                out=coef[:, 3 * k, :, b], in0=scl, scalar=1.0, in1=cols1[:, gcol:gcol + 8], op0=ALU.add, op1=ALU.mult),
                reads=[B_mod, B_cols], writes=[B_coef])
            sc.emit("dve", lambda e, k=k, b=b, sh=sh: e.tensor_copy(out=coef[:, 3 * k + 1, :, b], in_=sh),
                    reads=[B_mod], writes=[B_coef])
            gmul = 1.0 if k == 1 else 0.5
            sc.emit("dve", lambda e, k=k, b=b, gg=gg, gmul=gmul: e.tensor_scalar(
                out=coef[:, 3 * k + 2, :, b], in0=gg, scalar1=gmul, scalar2=None, op0=ALU.mult),
                reads=[B_mod], writes=[B_coef])

    sc.barrier()
    ar.reset(pmark)

    NT = 256
    NTILES = T // NT


    def norm_mod(k, xt, Bx, sq, Bsq, stt_, Bstt, rstd_, Brstd, ut, But, b, NT, fp32_u=False):
        sc.emit("act", lambda e: e.activation(
            out=sq[:].rearrange("p c n -> p (c n)"), in_=xt[:].rearrange("p c n -> p (c n)"), func=AF.Square),
            reads=[Bx], writes=[Bsq])
        for c in range(8):
            sc.emit("pe", lambda e, c=c: e.matmul(banks[4][:, 0:NT], lhsT=ones[:], rhs=sq[:, c, :],
                                                   start=(c == 0), stop=(c == 7)),
                    reads=[Bsq, B_const], writes=[bbuf[4]], inc=(c == 7))
        sc.emit("act", lambda e: e.activation(out=stt_[:], in_=banks[4][:, 0:NT], func=AF.Sqrt,
                                               scale=1.0 / D, bias=epsb[:]),
                reads=[bbuf[4], B_const2], writes=[Bstt])
        sc.emit("dve", lambda e: e.reciprocal(out=rstd_[:], in_=stt_[:]), reads=[Bstt], writes=[Brstd])
        sc.emit("dve", lambda e: e.tensor_tensor(out=sq[:], in0=xt[:], in1=rstd_[:].unsqueeze(1).to_broadcast([128, 8, NT]), op=ALU.mult),
                reads=[Bx, Brstd], writes=[Bsq])
        for c in range(8):
            sc.emit("act", lambda e, c=c: e.activation(
                out=(sq[:, c, :] if fp32_u else ut[:, c, :]), in_=sq[:, c, :], func=AF.Identity,
                scale=coef[:, 3 * k, c, b:b + 1], bias=coef[:, 3 * k + 1, c, b:b + 1]),
                reads=[Bsq, B_coef], writes=([Bsq] if fp32_u else [But]))
        if fp32_u:
            sc.emit("dve", lambda e: e.tensor_copy(out=ut[:], in_=sq[:]), reads=[Bsq], writes=[But])

    def ffn_phase(k, w_in_d, w_out_d, src_tokmajor, dst_tokmajor, src_fm, dst_fm):
        m0 = ar.mark()
        w_in = ar.tile([128, 8, 2 * DFF], BF16, "w_in")
        w_out = ar.tile([128, NJ, D], BF16, "w_out")
        B_win = [Buf() for _ in range(4)]
        B_wout = [Buf() for _ in range(2)]
        w_in_v = w_in_d.rearrange("(c p) n -> p c n", p=128)
        w_out_v = w_out_d.rearrange("(c p) n -> p c n", p=128)
        CW = 2 * DFF // 4
        for q in range(4):
            sc.dma("pool", lambda e, q=q: e.dma_start(out=w_in[:, :, q * CW:(q + 1) * CW], in_=w_in_v[:, :, q * CW:(q + 1) * CW]),
                   writes=[B_win[q]])
        for q in range(2):
            sc.dma("pool", lambda e, q=q: e.dma_start(out=w_out[:, q * 11:(q + 1) * 11, :], in_=w_out_v[:, q * 11:(q + 1) * 11, :]),
                   writes=[B_wout[q]])
        xin = [ar.tile([128, 2, D], F32, "xin%d" % i) for i in range(2)] if (src_tokmajor is not None or dst_tokmajor is not None) else None
        xT = [ar.tile([128, 8, NT], F32, "xT%d" % i) for i in range(2)]
        sq = ar.tile([128, 8, NT], F32, "sq")
        uT = [ar.tile([128, 8, NT], BF16, "uT%d" % i) for i in range(2)]
        g = ar.tile([128, NJ, NT], BF16, "g")
        stt = [ar.tile([128, NT], F32, "stt%d" % i) for i in range(2)]
        rstd = [ar.tile([128, NT], F32, "rstd%d" % i) for i in range(2)]
        sl = [ar.tile([128, NT], F32, "sl%d" % i) for i in range(2)]
        B_xin = [Buf(), Buf()]
        B_xT = [Buf(), Buf()]
        B_sq = Buf()
        B_uT = [Buf(), Buf()]
        B_g = [Buf() for _ in range(NJ)]
        B_stt = [Buf(), Buf()]
        B_rstd = [Buf(), Buf()]
        B_sl = [Buf(), Buf()]
        slc = 0
        def pre(it):
            nonlocal slc
            p = it % 2
            b = (it * NT) // S
            t0 = it * NT
            xt = xT[p]
            if src_tokmajor is not None:
                xi = xin[p]
                sc.dma("sp", lambda e, xi=xi, t0=t0: e.dma_start(
                    out=xi[:], in_=src_tokmajor[t0:t0 + NT, :].rearrange("(k p) d -> p k d", p=128)),
                    writes=[B_xin[p]])
                for c2 in range(4):
                    bk = 2 + (c2 % 2)
                    for cc in range(2):
                        c = 2 * c2 + cc
                        for kb in range(2):
                            sc.emit("pe", lambda e, xi=xi, bk=bk, cc=cc, kb=kb, c=c: e.transpose(
                                out=banks[bk][:, cc * 256 + kb * 128: cc * 256 + kb * 128 + 128],
                                in_=xi[:, kb, c * 128:(c + 1) * 128], identity=ident[:]),
                                reads=[B_xin[p], B_const], writes=[bbuf[bk]], inc=(cc == 1 and kb == 1))
                    eng = "act" if c2 % 2 == 0 else "dve"
                    if eng == "act":
                        sc.emit("act", lambda e, xt=xt, bk=bk, c2=c2: e.copy(
                            out=xt[:, 2 * c2:2 * c2 + 2, :].rearrange("p c n -> p (c n)"), in_=banks[bk][:, :]),
                            reads=[bbuf[bk]], writes=[B_xT[p]])
                    else:
                        sc.emit("dve", lambda e, xt=xt, bk=bk, c2=c2: e.tensor_copy(
                            out=xt[:, 2 * c2:2 * c2 + 2, :].rearrange("p c n -> p (c n)"), in_=banks[bk][:, :]),
                            reads=[bbuf[bk]], writes=[B_xT[p]])
            else:
                sc.dma("sp", lambda e, xt=xt, t0=t0: e.dma_start(
                    out=xt[:], in_=src_fm[:, :, t0:t0 + NT].rearrange("c p n -> p c n")), writes=[B_xT[p]])
            norm_mod(k, xt, B_xT[p], sq, B_sq, stt[p], B_stt[p], rstd[p], B_rstd[p], uT[p], B_uT[p], b, NT)
        def main(it):
            nonlocal slc
            p = it % 2
            b = (it * NT) // S
            t0 = it * NT
            xt = xT[p]
            for j in range(NJ):
                bk = 5 + (j % 2)
                for half in range(2):
                    col0 = half * DFF + j * 128
                    q = col0 // CW
                    assert (col0 + 127) // CW == q
                    for c in range(8):
                        sc.emit("pe", lambda e, bk=bk, half=half, c=c, col0=col0, p=p: e.matmul(
                            banks[bk][:, half * NT:(half + 1) * NT], lhsT=w_in[:, c, col0:col0 + 128], rhs=uT[p][:, c, :],
                            start=(c == 0), stop=(c == 7)),
                            reads=[B_win[q], B_uT[p]], writes=[bbuf[bk]], inc=(c == 7 and half == 1))
                s_ = sl[slc % 2]
                bs = B_sl[slc % 2]
                slc += 1
                sc.emit("act", lambda e, bk=bk, s_=s_: e.activation(out=s_[:], in_=banks[bk][:, 0:NT], func=AF.Silu),
                        reads=[bbuf[bk]], writes=[bs])
                sc.emit("dve", lambda e, bk=bk, s_=s_, j=j: e.tensor_tensor(out=g[:, j, :], in0=s_[:], in1=banks[bk][:, NT:2 * NT], op=ALU.mult),
                        reads=[bbuf[bk], bs], writes=[B_g[j]])
            for c2 in range(4):
                bk = 2 + (c2 % 2)
                for cc in range(2):
                    c = 2 * c2 + cc
                    for j in range(NJ):
                        sc.emit("pe", lambda e, bk=bk, cc=cc, c=c, j=j: e.matmul(
                            banks[bk][:, cc * NT:(cc + 1) * NT], lhsT=w_out[:, j, c * 128:(c + 1) * 128], rhs=g[:, j, :],
                            start=(j == 0), stop=(j == NJ - 1)),
                            reads=[B_wout[j // 11], B_g[j]], writes=[bbuf[bk]], inc=(j == NJ - 1 and cc == 1))
                for cc in range(2):
                    c = 2 * c2 + cc
                    sc.emit("dve", lambda e, bk=bk, cc=cc, c=c, xt=xt, b=b: e.scalar_tensor_tensor(
                        out=xt[:, c, :], in0=banks[bk][:, cc * NT:(cc + 1) * NT], scalar=coef[:, 3 * k + 2, c, b:b + 1],
                        in1=xt[:, c, :], op0=ALU.mult, op1=ALU.add),
                        reads=[bbuf[bk], B_coef, B_xT[p]], writes=[B_xT[p]])
            if dst_fm is not None:
                sc.dma("pool", lambda e, xt=xt, t0=t0: e.dma_start(
                    out=dst_fm[:, :, t0:t0 + NT].rearrange("c p n -> p c n"), in_=xt[:]), reads=[B_xT[p]])
            else:
                xi = xin[p]
                for kb in range(2):
                    for c4 in range(2):
                        bk = 2 + ((kb * 2 + c4) % 2)
                        for cq in range(4):
                            c = c4 * 4 + cq
                            sc.emit("pe", lambda e, xt=xt, bk=bk, cq=cq, kb=kb, c=c: e.transpose(
                                out=banks[bk][:, cq * 128:(cq + 1) * 128],
                                in_=xt[:, c, kb * 128:(kb + 1) * 128], identity=ident[:]),
                                reads=[B_xT[p], B_const], writes=[bbuf[bk]], inc=(cq == 3))
                        if c4 == 0:
                            sc.emit("act", lambda e, xi=xi, bk=bk, kb=kb, c4=c4: e.copy(
                                out=xi[:, kb, c4 * 512:(c4 + 1) * 512], in_=banks[bk][:, :]),
                                reads=[bbuf[bk]], writes=[B_xin[p]])
                        else:
                            sc.emit("dve", lambda e, xi=xi, bk=bk, kb=kb, c4=c4: e.tensor_copy(
                                out=xi[:, kb, c4 * 512:(c4 + 1) * 512], in_=banks[bk][:, :]),
                                reads=[bbuf[bk]], writes=[B_xin[p]])
                out_evs.append(sc.dma("pool", lambda e, xi=xi, t0=t0: e.dma_start(
                    out=dst_tokmajor[t0:t0 + NT, :].rearrange("(k p) d -> p k d", p=128), in_=xi[:]),
                    reads=[B_xin[p]]))
        NTL_ = NTILES
        pre(0)
        for it in range(NTL_):
            if it + 1 < NTL_:
                pre(it + 1)
            main(it)
        sc.barrier()
        ar.reset(m0)

    out_evs = []
    epsb = ar.tile([128, 1], F32, "epsb")
    B_const2 = Buf()
    sc.emit("pool", lambda e: e.memset(epsb[:], EPS), writes=[B_const2])
    pmark = ar.mark()

    qkm_s = scratch("qkm_s", [8, 128, T], BF16)
    qf_s = scratch("qf_s", [8, 128, T], BF16)
    kf_s = scratch("kf_s", [8, 128, T], BF16)
    ga_s = scratch("ga_s", [8, 128, T], BF16)
    gb_s = scratch("gb_s", [8, 128, T], BF16)
    vm_s = scratch("vm_s", [T, D], BF16)
    om_s = scratch("om_s", [T, D], BF16)
    vf_s = scratch("vf_s", [T, 16, 128], BF16)
    gm_s = scratch("gm_s", [2, 4, T], F32)
    gf_s = scratch("gf_s", [16, T], F32)

    w_mix_v = w_mix.rearrange("(c p) n -> p c n", p=128)

    def mix_phase_fm():
        NT = 256
        m0 = ar.mark()
        wfm = ar.tile([128, 8, 5120], BF16, "wfm")
        wg = ar.tile([128, 8, 24], F32, "wg")
        B_w = [Buf() for _ in range(5)]
        B_wg = Buf()
        segs = [(0, 0, 1024), (1024, O_QF, 1024), (2048, O_KF, 1024), (3072, O_GA, 1024), (4096, O_GB, 1024)]
        for i, (d0, s0, n) in enumerate(segs):
            sc.dma("pool", lambda e, d0=d0, s0=s0, n=n: e.dma_start(out=wfm[:, :, d0:d0 + n], in_=w_mix_v[:, :, s0:s0 + n]),
                   writes=[B_w[i]])
        sc.dma("sp", lambda e: e.dma_start(out=wg[:, :, 0:8], in_=w_mix_v[:, :, O_IM:O_IM + 8]), writes=[B_wg])
        sc.dma("sp", lambda e: e.dma_start(out=wg[:, :, 8:24], in_=w_mix_v[:, :, O_FF:O_FF + 16]), writes=[B_wg])
        gbias = ar.tile([16, 3], F32, "gbias")
        B_gb = Buf()
        sc.dma("sp", lambda e: e.dma_start(out=gbias[0:4, 0:1], in_=b_mix[O_IM:O_IM + 4].rearrange("(p o) -> p o", o=1)), writes=[B_gb])
        sc.dma("sp", lambda e: e.dma_start(out=gbias[0:4, 1:2], in_=b_mix[O_FM:O_FM + 4].rearrange("(p o) -> p o", o=1)), writes=[B_gb])
        sc.dma("sp", lambda e: e.dma_start(out=gbias[0:16, 2:3], in_=b_mix[O_FF:O_FF + 16].rearrange("(p o) -> p o", o=1)), writes=[B_gb])
        xT = [ar.tile([128, 8, NT], F32, "mxT%d" % i) for i in range(2)]
        sq = ar.tile([128, 8, NT], F32, "msq")
        uT = [ar.tile([128, 8, NT], BF16, "muT%d" % i) for i in range(2)]
        stt = [ar.tile([128, NT], F32, "mstt%d" % i) for i in range(2)]
        rstd = [ar.tile([128, NT], F32, "mrstd%d" % i) for i in range(2)]
        R = [ar.tile([128, 8, NT + 3], F32, "R%d" % i) for i in range(2)]
        acc = [ar.tile([128, NT], F32, "acc%d" % i) for i in range(2)]
        tq = [ar.tile([128, 2, NT], F32, "tq%d" % i) for i in range(2)]
        tsq = [ar.tile([128, 2, NT], F32, "tsq%d" % i) for i in range(2)]
        tsd = [ar.tile([128, 2, NT], F32, "tsd%d" % i) for i in range(2)]
        st = {n: [ar.tile([128, 8, NT], BF16, "st_%s%d" % (n, i)) for i in range(2)] for n in ("qkm", "qf", "kf", "ga", "gb")}
        stg = [ar.tile([16, 3, NT], F32, "stg%d" % i) for i in range(2)]
        B_xT = [Buf(), Buf()]; B_sq = Buf(); B_uT = [Buf(), Buf()]; B_stt = [Buf(), Buf()]; B_rstd = [Buf(), Buf()]
        B_R = [Buf(), Buf()]; B_acc = [Buf(), Buf()]; B_tq = [Buf(), Buf()]; B_tsq = [Buf(), Buf()]; B_tsd = [Buf(), Buf()]
        B_st = {n: [Buf(), Buf()] for n in st}
        B_stg = [Buf(), Buf()]
        dsts = {"qkm": qkm_s, "qf": qf_s, "kf": kf_s, "ga": ga_s, "gb": gb_s}
        kscale = 128.0 ** -0.5
        cnt = 0
        def pre(it):
            nonlocal cnt
            p = it % 2
            b = (it * NT) // S
            t0 = it * NT
            xt = xT[p]
            sc.dma("sp", lambda e, xt=xt, t0=t0: e.dma_start(
                out=xt[:], in_=x1s[:, :, t0:t0 + NT].rearrange("c p n -> p c n")), writes=[B_xT[p]])
            norm_mod(1, xt, B_xT[p], sq, B_sq, stt[p], B_stt[p], rstd[p], B_rstd[p], uT[p], B_uT[p], b, NT, fp32_u=True)
            ut = uT[p]
            for gi, (c0, n, bk, col) in enumerate(((0, 4, 7, 0), (4, 4, 7, NT), (8, 16, 4, NT))):
                for c in range(8):
                    sc.emit("pe", lambda e, c=c, c0=c0, n=n, bk=bk, col=col: e.matmul(
                        banks[bk][0:n, col:col + NT], lhsT=wg[:, c, c0:c0 + n], rhs=sq[:, c, :], start=(c == 0), stop=(c == 7)),
                        reads=[B_wg, B_sq], writes=[bbuf[bk]], inc=(c == 7))
                sc.emit("dve", lambda e, gi=gi, n=n, bk=bk, col=col, p=p: e.tensor_scalar(
                    out=stg[p][0:n, gi, :], in0=banks[bk][0:n, col:col + NT], scalar1=gbias[0:n, gi:gi + 1], scalar2=None, op0=ALU.add),
                    reads=[bbuf[bk], B_gb], writes=[B_stg[p]])
            sc.dma("pool", lambda e, p=p, t0=t0: e.dma_start(out=gm_s[:, :, t0:t0 + NT].rearrange("g h n -> h g n"), in_=stg[p][0:4, 0:2, :]),
                   reads=[B_stg[p]])
            sc.dma("pool", lambda e, p=p, t0=t0: e.dma_start(out=gf_s[:, t0:t0 + NT], in_=stg[p][0:16, 2, :]), reads=[B_stg[p]])
        def main(it):
            nonlocal cnt
            p = it % 2
            b = (it * NT) // S
            t0 = it * NT
            ut = uT[p]
            Rp, Rq = R[p], R[1 - p]
            if (t0 % S) == 0:
                sc.emit("pool", lambda e, Rp=Rp: e.memset(Rp[:, :, 0:3], 0.0), writes=[B_R[p]])
            else:
                sc.emit("pool", lambda e, Rp=Rp, Rq=Rq: e.tensor_copy(out=Rp[:, :, 0:3], in_=Rq[:, :, NT:NT + 3]),
                        reads=[B_R[1 - p]], writes=[B_R[p]])
            groups = []

            def proj(wcol0, bk, half):
                wseg = wcol0 // 1024
                for c in range(8):
                    sc.emit("pe", lambda e, c=c: e.matmul(
                        banks[bk][:, half * NT:(half + 1) * NT], lhsT=wfm[:, c, wcol0:wcol0 + 128], rhs=ut[:, c, :],
                        start=(c == 0), stop=(c == 7)),
                        reads=[B_w[wseg], B_uT[p]], writes=[bbuf[bk]], inc=(c == 7))

            gi = [0]

            def nextbank():
                bk = gi[0] % 4
                gi[0] += 1
                return bk

            for c in range(8):
                bk = nextbank()

                def pe_fn(c=c, bk=bk):
                    proj(c * 128, bk, 0)

                def cons_fn(c=c, bk=bk):
                    nonlocal cnt
                    sc.emit("act", lambda e: e.activation(
                        out=Rp[:, c, 3:3 + NT], in_=banks[bk][:, 0:NT], func=AF.Identity, bias=cols2[:, 8 + c:9 + c]),
                        reads=[bbuf[bk], B_cols], writes=[B_R[p]])
                    a_ = acc[cnt % 2]; Ba = B_acc[cnt % 2]; cnt += 1
                    sc.emit("dve", lambda e: e.tensor_scalar(
                        out=a_[:], in0=Rp[:, c, 0:NT], scalar1=cols1[:, 96 + c:97 + c], scalar2=cols2[:, c:c + 1], op0=ALU.mult, op1=ALU.add),
                        reads=[B_R[p], B_cols], writes=[Ba])
                    for j in (1, 2, 3):
                        sc.emit("dve", lambda e, j=j: e.scalar_tensor_tensor(
                            out=a_[:], in0=Rp[:, c, j:j + NT], scalar=cols1[:, 96 + 8 * j + c:97 + 8 * j + c], in1=a_[:], op0=ALU.mult, op1=ALU.add),
                            reads=[B_R[p], B_cols, Ba], writes=[Ba])
                    sc.emit("act", lambda e: e.activation(out=st["qkm"][p][:, c, :], in_=a_[:], func=AF.Silu),
                            reads=[Ba], writes=[B_st["qkm"][p]])
                    if c == 7:
                        sc.emit("dve", lambda e: e.tensor_scalar(
                            out=st["qkm"][p][:, 4:8, :], in0=st["qkm"][p][:, 4:8, :], scalar1=kscale, scalar2=None, op0=ALU.mult),
                            reads=[B_st["qkm"][p]], writes=[B_st["qkm"][p]])
                groups.append((pe_fn, cons_fn))
            for name, wc0, bcol, gcol in (("qf", 1024, 32, 72), ("kf", 2048, 40, 80)):
                for c2 in range(4):
                    bk = nextbank()

                    def pe_fn(wc0=wc0, c2=c2, bk=bk):
                        for cc in range(2):
                            proj(wc0 + (2 * c2 + cc) * 128, bk, cc)

                    i2 = cnt % 2; cnt += 1
                    bb = 5 + i2

                    def cons_fn(name=name, bcol=bcol, gcol=gcol, c2=c2, bk=bk, i2=i2, bb=bb):
                        pv = banks[bk][:, :].rearrange("p (c n) -> p c n", c=2)
                        sc.emit("dve", lambda e: e.tensor_tensor(
                            out=tq[i2][:], in0=pv, in1=cols2[:, bcol + 2 * c2:bcol + 2 * c2 + 2].unsqueeze(2).to_broadcast([128, 2, NT]), op=ALU.add),
                            reads=[bbuf[bk], B_cols], writes=[B_tq[i2]])
                        sc.emit("act", lambda e: e.activation(out=tsq[i2][:].rearrange("p c n -> p (c n)"),
                                                              in_=tq[i2][:].rearrange("p c n -> p (c n)"), func=AF.Square),
                                reads=[B_tq[i2]], writes=[B_tsq[i2]])
                        sc.emit("pe", lambda e: e.matmul(banks[bb][:, :], lhsT=bones[:], rhs=tsq[i2][:].rearrange("p c n -> p (c n)"),
                                                         start=True, stop=True),
                                reads=[B_tsq[i2], B_const], writes=[bbuf[bb]])

                    def cons2_fn(name=name, bcol=bcol, gcol=gcol, c2=c2, bk=bk, i2=i2, bb=bb):
                        sc.emit("act", lambda e: e.activation(out=tsd[i2][:].rearrange("p c n -> p (c n)"), in_=banks[bb][:, :],
                                                              func=AF.Ln, scale=1.0 / 64, bias=epsb[:]),
                                reads=[bbuf[bb], B_const2], writes=[B_tsd[i2]])
                        sc.emit("act", lambda e: e.activation(out=tsd[i2][:].rearrange("p c n -> p (c n)"),
                                                              in_=tsd[i2][:].rearrange("p c n -> p (c n)"), func=AF.Exp, scale=-0.5),
                                reads=[B_tsd[i2]], writes=[B_tsd[i2]])
                        sc.emit("dve", lambda e: e.tensor_tensor(out=tq[i2][:], in0=tq[i2][:], in1=tsd[i2][:], op=ALU.mult),
                                reads=[B_tq[i2], B_tsd[i2]], writes=[B_tq[i2]])
                        sc.emit("dve", lambda e: e.tensor_tensor(
                            out=st[name][p][:, 2 * c2:2 * c2 + 2, :], in0=tq[i2][:],
                            in1=cols2[:, gcol + 2 * c2:gcol + 2 * c2 + 2].unsqueeze(2).to_broadcast([128, 2, NT]), op=ALU.mult),
                            reads=[B_tq[i2], B_cols], writes=[B_st[name][p]])
                    groups.append((pe_fn, cons_fn, cons2_fn))
            for name, wc0, bcol in (("ga", 3072, 56), ("gb", 4096, 64)):
                for c2 in range(4):
                    bk = nextbank()

                    def pe_fn(wc0=wc0, c2=c2, bk=bk):
                        for cc in range(2):
                            proj(wc0 + (2 * c2 + cc) * 128, bk, cc)

                    def cons_fn(name=name, bcol=bcol, c2=c2, bk=bk):
                        for cc in range(2):
                            c = 2 * c2 + cc
                            sc.emit("act", lambda e, c=c, cc=cc: e.activation(
                                out=st[name][p][:, c, :], in_=banks[bk][:, cc * NT:(cc + 1) * NT], func=AF.Sigmoid, bias=cols2[:, bcol + c:bcol + c + 1]),
                                reads=[bbuf[bk], B_cols], writes=[B_st[name][p]])
                    groups.append((pe_fn, cons_fn))
            LAG = 2
            for g in range(len(groups) + LAG + 1):
                if g < len(groups):
                    groups[g][0]()
                if LAG <= g < len(groups) + LAG:
                    groups[g - LAG][1]()
                if g >= LAG + 1 and len(groups[g - LAG - 1]) > 2:
                    groups[g - LAG - 1][2]()
            for name in st:
                sc.dma("pool", lambda e, name=name, p=p, t0=t0: e.dma_start(
                    out=dsts[name][:, :, t0:t0 + NT].rearrange("c p n -> p c n"), in_=st[name][p][:]), reads=[B_st[name][p]])
        NTL_ = T // NT
        pre(0)
        for it in range(NTL_):
            if it + 1 < NTL_:
                pre(it + 1)
            main(it)
        sc.barrier()
        ar.reset(m0)

    def mix_phase_tm():
        NT = 256
        m0 = ar.mark()
        wtm = ar.tile([128, 8, 3072], BF16, "wtm")
        B_w = [Buf() for _ in range(3)]
        for i, (d0, s0) in enumerate(((0, O_VM), (1024, O_OM), (2048, O_VF))):
            sc.dma("pool", lambda e, d0=d0, s0=s0: e.dma_start(out=wtm[:, :, d0:d0 + 1024], in_=w_mix_v[:, :, s0:s0 + 1024]),
                   writes=[B_w[i]])
        brow = ar.tile([128, 3072], F32, "brow")
        B_brow = Buf()
        for i, s0 in enumerate((O_VM, O_OM, O_VF)):
            sc.dma("sp", lambda e, i=i, s0=s0: e.dma_start(out=brow[:, i * 1024:(i + 1) * 1024], in_=b_mix[s0:s0 + 1024].partition_broadcast(128)),
                   writes=[B_brow])
        xT = [ar.tile([128, 8, NT], F32, "cxT%d" % i) for i in range(2)]
        sq = ar.tile([128, 8, NT], F32, "csq")
        uT = [ar.tile([128, 8, NT], BF16, "cuT%d" % i) for i in range(2)]
        stt = [ar.tile([128, NT], F32, "cstt%d" % i) for i in range(2)]
        rstd = [ar.tile([128, NT], F32, "crstd%d" % i) for i in range(2)]
        svm = [ar.tile([128, 2, D], BF16, "svm%d" % i) for i in range(2)]
        som = [ar.tile([128, 2, D], BF16, "som%d" % i) for i in range(2)]
        svf = [ar.tile([128, 2, 16, 128], BF16, "svf%d" % i) for i in range(2)]
        tmp = [ar.tile([128, 512], F32, "ctmp%d" % i) for i in range(2)]
        B_xT = [Buf(), Buf()]; B_sq = Buf(); B_uT = [Buf(), Buf()]; B_stt = [Buf(), Buf()]; B_rstd = [Buf(), Buf()]
        B_svm = [Buf(), Buf()]; B_som = [Buf(), Buf()]; B_svf = [Buf(), Buf()]; B_tmp = [Buf(), Buf()]
        for i in range(2):
            sc.emit("pool", lambda e, i=i: e.memset(svf[i][:].rearrange("p k h d -> p (k h d)"), 1.0), writes=[B_svf[i]])
        cnt = 0
        def pre(it):
            nonlocal cnt
            p = it % 2
            b = (it * NT) // S
            t0 = it * NT
            xt = xT[p]
            sc.dma("sp", lambda e, xt=xt, t0=t0: e.dma_start(
                out=xt[:], in_=x1s[:, :, t0:t0 + NT].rearrange("c p n -> p c n")), writes=[B_xT[p]])
            norm_mod(1, xt, B_xT[p], sq, B_sq, stt[p], B_stt[p], rstd[p], B_rstd[p], uT[p], B_uT[p], b, NT)
            ut = uT[p]
        def main(it):
            nonlocal cnt
            p = it % 2
            b = (it * NT) // S
            t0 = it * NT
            xt = xT[p]
            ut = uT[p]
            for kb in range(2):
                for grp in range(6):
                    bk = 2 + (cnt % 4); cnt += 1
                    for c in range(8):
                        sc.emit("pe", lambda e, c=c, bk=bk, grp=grp, kb=kb, ut=ut: e.matmul(
                            banks[bk][:, :], lhsT=ut[:, c, kb * 128:(kb + 1) * 128], rhs=wtm[:, c, grp * 512:(grp + 1) * 512],
                            start=(c == 0), stop=(c == 7)),
                            reads=[B_w[grp // 2], B_uT[p]], writes=[bbuf[bk]], inc=(c == 7))
                    br = brow[:, grp * 512:(grp + 1) * 512]
                    if grp < 2:
                        sc.emit("dve", lambda e, bk=bk, grp=grp, kb=kb, p=p, br=br: e.tensor_tensor(
                            out=svm[p][:, kb, grp * 512:(grp + 1) * 512], in0=banks[bk][:, :], in1=br, op=ALU.add),
                            reads=[bbuf[bk], B_brow], writes=[B_svm[p]])
                    elif grp < 4:
                        i2 = cnt % 2
                        sc.emit("dve", lambda e, bk=bk, i2=i2, br=br: e.tensor_tensor(out=tmp[i2][:], in0=banks[bk][:, :], in1=br, op=ALU.add),
                                reads=[bbuf[bk], B_brow], writes=[B_tmp[i2]])
                        sc.emit("act", lambda e, i2=i2, grp=grp, kb=kb, p=p: e.activation(
                            out=som[p][:, kb, (grp - 2) * 512:(grp - 1) * 512], in_=tmp[i2][:], func=AF.Sigmoid),
                            reads=[B_tmp[i2]], writes=[B_som[p]])
                    else:
                        h0 = (grp - 4) * 4
                        svv = svf[p][:].rearrange("p k (hp two) d -> p k hp two d", two=2)
                        pv4 = banks[bk][:, :].rearrange("p (hp two d) -> p hp two d", two=2, d=64)
                        br4 = br.rearrange("p (hp two d) -> p hp two d", two=2, d=64)
                        for hh in range(2):
                            sc.emit("dve", lambda e, kb=kb, h0=h0, hh=hh, svv=svv, pv4=pv4, br4=br4: e.tensor_tensor(
                                out=svv[:, kb, h0:h0 + 4, hh, 64 * hh:64 * hh + 64], in0=pv4[:, :, hh, :],
                                in1=br4[:, :, hh, :], op=ALU.add),
                                reads=[bbuf[bk], B_brow], writes=[B_svf[p]])
            sc.dma("pool", lambda e, p=p, t0=t0: e.dma_start(out=vm_s[t0:t0 + NT, :].rearrange("(k p) d -> p k d", p=128), in_=svm[p][:]),
                   reads=[B_svm[p]])
            sc.dma("pool", lambda e, p=p, t0=t0: e.dma_start(out=om_s[t0:t0 + NT, :].rearrange("(k p) d -> p k d", p=128), in_=som[p][:]),
                   reads=[B_som[p]])
            sc.dma("pool", lambda e, p=p, t0=t0: e.dma_start(out=vf_s[t0:t0 + NT, :, :].rearrange("(k p) h d -> p k h d", p=128), in_=svf[p][:]),
                   reads=[B_svf[p]])
        NTL_ = T // NT
        pre(0)
        for it in range(NTL_):
            if it + 1 < NTL_:
                pre(it + 1)
            main(it)
        sc.barrier()
        ar.reset(m0)

    ya_s = scratch("ya_s", [8, 128, T], BF16)
    yb_s = scratch("yb_s", [8, 128, T], BF16)
    cst2 = din("consts2", [16, 2, S])

    identb = ar.tile([128, 128], BF16, "identb")
    maskb = ar.tile([128, 128], BF16, "maskb")
    B_cb = Buf()
    sc.emit("pool", lambda e: e.tensor_copy(out=identb[:], in_=ident[:]), reads=[B_const], writes=[B_cb])
    sc.emit("pool", lambda e: e.tensor_copy(out=maskb[:], in_=maskadd[:]), reads=[B_const], writes=[B_cb])

    def logsigmoid(n, f_, lf_, t1, t2, Bf, Blf, Bt1, Bt2):
        sc.emit("act", lambda e: e.activation(out=t1[0:n, :], in_=f_[0:n, :], func=AF.Abs), reads=[Bf], writes=[Bt1])
        sc.emit("act", lambda e: e.activation(out=t1[0:n, :], in_=t1[0:n, :], func=AF.Exp, scale=-1.0), reads=[Bt1], writes=[Bt1])
        sc.emit("act", lambda e: e.activation(out=t1[0:n, :], in_=t1[0:n, :], func=AF.Ln, bias=ones[0:n, 0:1]), reads=[Bt1, B_const], writes=[Bt1])
        sc.emit("dve", lambda e: e.tensor_single_scalar(out=t2[0:n, :], in_=f_[0:n, :], scalar=0.0, op=ALU.min), reads=[Bf], writes=[Bt2])
        sc.emit("dve", lambda e: e.tensor_tensor(out=lf_[0:n, :], in0=t2[0:n, :], in1=t1[0:n, :], op=ALU.subtract), reads=[Bt1, Bt2], writes=[Blf])

    def fox_phase():
        m0 = ar.mark()
        onesr = ar.tile([16, S], F32, "onesr")
        B_or = Buf()
        sc.emit("pool", lambda e: e.memset(onesr[:], 1.0), writes=[B_or])
        fg = ar.tile([16, S], F32, "fg"); lf = ar.tile([16, S], F32, "lf"); t1 = ar.tile([16, S], F32, "t1"); t2 = ar.tile([16, S], F32, "t2")
        Fc = ar.tile([16, S], F32, "Fc")
        P3 = ar.tile([16, 3, S], BF16, "P3"); N3 = ar.tile([16, 3, S], BF16, "N3")
        B_fg = Buf(); B_lf = Buf(); B_t1 = Buf(); B_t2 = Buf(); B_F = Buf(); B_P3 = Buf(); B_N3 = Buf()
        onesb = ar.tile([3, S], BF16, "onesb"); B_onesb = Buf()
        sc.emit("pool", lambda e: e.memset(onesb[:], 1.0), writes=[B_onesb])
        qz = [[ar.tile([128, S], BF16, "qz%d%d" % (i, h)) for h in range(2)] for i in range(2)]
        kz = [[ar.tile([128, S], BF16, "kz%d%d" % (i, h)) for h in range(2)] for i in range(2)]
        B_qz = [[Buf(), Buf()], [Buf(), Buf()]]; B_kz = [[Buf(), Buf()], [Buf(), Buf()]]
        for i in range(2):
            for h in range(2):
                sc.emit("pool", lambda e, i=i, h=h: e.memset(qz[i][h][64:128, :], 0.0), writes=[B_qz[i][h]])
                sc.emit("pool", lambda e, i=i, h=h: e.memset(kz[i][h][64:128, :], 0.0), writes=[B_kz[i][h]])
                sc.dma("sp", lambda e, i=i, h=h: e.dma_start(out=qz[i][h][67:70, :], in_=onesb[:]), reads=[B_onesb], writes=[B_qz[i][h]])
                sc.dma("sp", lambda e, i=i, h=h: e.dma_start(out=kz[i][h][64:67, :], in_=onesb[:]), reads=[B_onesb], writes=[B_kz[i][h]])
        vx = [ar.tile([128, 16, 2, 128], BF16, "fvx%d" % i) for i in range(2)]
        yb = [ar.tile([128, S], BF16, "fyb%d" % i) for i in range(2)]
        B_v = [Buf(), Buf()]; B_yb = [Buf(), Buf()]
        NP = 4
        Pt = [ar.tile([128, 512], BF16, "Pt%d" % i) for i in range(NP)]
        B_Pt = [Buf() for _ in range(NP)]
        recT = [ar.tile([128, 512], F32, "recT%d" % i) for i in range(2)]
        rec2 = [ar.tile([128, 512], F32, "rec2%d" % i) for i in range(2)]
        B_recT = [Buf(), Buf()]; B_rec2 = [Buf(), Buf()]
        pi = 0; si = 0; oi = 0; ri = 0; it = 0
        work = []
        chunk_first = []
        chunk_loads = []
        for b in range(BL):
            tb0 = b * S
            def prep(tb0=tb0):
                sc.dma("sp", lambda e, tb0=tb0: e.dma_start(out=fg[:], in_=gf_s[:, tb0:tb0 + S]), writes=[B_fg])
                logsigmoid(16, fg, lf, t1, t2, B_fg, B_lf, B_t1, B_t2)
                sc.emit("dve", lambda e: e.tensor_tensor_scan(out=Fc[:], data0=onesr[:], data1=lf[:], initial=0.0, op0=ALU.mult, op1=ALU.add),
                        reads=[B_or, B_lf], writes=[B_F])
                sc.emit("dve", lambda e: e.tensor_scalar(out=P3[:, 0, :], in0=Fc[:], scalar1=8.0, scalar2=None, op0=ALU.mult), reads=[B_F], writes=[B_P3])
                sc.emit("dve", lambda e: e.scalar_tensor_tensor(out=t1[:], in0=Fc[:], scalar=8.0, in1=P3[:, 0, :], op0=ALU.mult, op1=ALU.subtract),
                        reads=[B_F, B_P3], writes=[B_t1])
                sc.emit("dve", lambda e: e.tensor_copy(out=P3[:, 1, :], in_=t1[:]), reads=[B_t1], writes=[B_P3])
                sc.emit("dve", lambda e: e.tensor_tensor(out=t2[:], in0=t1[:], in1=P3[:, 1, :], op=ALU.subtract), reads=[B_t1, B_P3], writes=[B_t2])
                sc.emit("dve", lambda e: e.tensor_copy(out=P3[:, 2, :], in_=t2[:]), reads=[B_t2], writes=[B_P3])
                sc.emit("dve", lambda e: e.tensor_scalar(out=N3[:].rearrange("p j n -> p (j n)"), in0=P3[:].rearrange("p j n -> p (j n)"),
                                                         scalar1=-1.0, scalar2=None, op0=ALU.mult), reads=[B_P3], writes=[B_N3])
            for c in range(8):
                p = it % 2; it += 1

                def loads(p=p, c=c, tb0=tb0, prep=prep):
                    if c == 0:
                        prep()
                    for hh in range(2):
                        sc.dma("sp", lambda e, hh=hh: e.dma_start(out=qz[p][hh][0:64, :], in_=qf_s[c, 64 * hh:64 * hh + 64, tb0:tb0 + S]),
                               writes=[B_qz[p][hh]])
                        sc.dma("sp", lambda e, hh=hh: e.dma_start(out=kz[p][hh][0:64, :], in_=kf_s[c, 64 * hh:64 * hh + 64, tb0:tb0 + S]),
                               writes=[B_kz[p][hh]])
                        sc.dma("sp", lambda e, hh=hh: e.dma_start(out=qz[p][hh][64:67, :], in_=P3[2 * c + hh:2 * c + hh + 1, :, :]),
                               reads=[B_P3], writes=[B_qz[p][hh]])
                        sc.dma("sp", lambda e, hh=hh: e.dma_start(out=kz[p][hh][67:70, :], in_=N3[2 * c + hh:2 * c + hh + 1, :, :]),
                               reads=[B_N3], writes=[B_kz[p][hh]])
                    sc.dma("sp", lambda e: e.dma_start(
                        out=vx[p][:], in_=vf_s[tb0:tb0 + S, 2 * c:2 * c + 2, :].rearrange("(k p) h d -> p k h d", p=128)), writes=[B_v[p]])
                chunk_first.append(len(work)); chunk_loads.append(loads)
                for hh in range(2):
                    pb = 64 * hh
                    for tb in range(4):
                        t0 = 512 * tb
                        bo = 4 + (oi % 2); oi += 1
                        nsb = 4 * (tb + 1)
                        for sb in range(nsb):
                            s0 = 128 * sb
                            diag = s0 >= t0
                            col0 = (s0 - t0) if diag else 0
                            bs = si % 4; si += 1
                            pt = Pt[pi % NP]; Bp = B_Pt[pi % NP]; pi += 1

                            def front(bs=bs, col0=col0, p=p, pb=pb, s0=s0, t0=t0, hh=hh, diag=diag, pt=pt, Bp=Bp):
                                sc.emit("pe", lambda e: e.matmul(
                                    banks[bs][:, col0:512], lhsT=kz[p][hh][:, s0:s0 + 128], rhs=qz[p][hh][:, t0 + col0:t0 + 512],
                                    start=True, stop=(not diag)), reads=[B_kz[p][hh], B_qz[p][hh]], writes=[bbuf[bs]], inc=(not diag))
                                if diag:
                                    sc.emit("pe", lambda e: e.matmul(
                                        banks[bs][:, col0:col0 + 128], lhsT=identb[:], rhs=maskb[:], start=False, stop=True),
                                        reads=[B_cb], writes=[bbuf[bs]])
                                sc.emit("act", lambda e: e.activation(
                                    out=pt[:, col0:512], in_=banks[bs][:, col0:512], func=AF.Exp, scale=0.125),
                                    reads=[bbuf[bs]], writes=[Bp])

                            last_of_group = (sb == nsb - 1)
                            last_of_chunk = last_of_group and hh == 1 and tb == 3
                            if last_of_group:
                                r = ri % 2; ri += 1
                            else:
                                r = None

                            def back(bo=bo, col0=col0, p=p, sb=sb, hh=hh, pt=pt, Bp=Bp, nsb=nsb, r=r, t0=t0, c=c, tb0=tb0,
                                     last_of_group=last_of_group, last_of_chunk=last_of_chunk):
                                sc.emit("pe", lambda e: e.matmul(
                                    banks[bo][:, col0:512], lhsT=vx[p][:, sb, hh, :], rhs=pt[:, col0:512],
                                    start=(sb == 0), stop=(sb == nsb - 1)), reads=[B_v[p], Bp], writes=[bbuf[bo]], inc=(sb == nsb - 1))
                                if last_of_group:
                                    po, pd = (0, 64) if hh == 0 else (64, 0)
                                    sc.emit("dve", lambda e: e.reciprocal(out=recT[r][pd:pd + 64, :], in_=banks[bo][pd:pd + 64, :]),
                                            reads=[bbuf[bo]], writes=[B_recT[r]])
                                    sc.emit("pool", lambda e: e.tensor_copy(out=rec2[r][po:po + 64, :], in_=recT[r][pd:pd + 64, :]),
                                            reads=[B_recT[r]], writes=[B_rec2[r]])
                                    sc.emit("dve", lambda e: e.tensor_tensor(
                                        out=yb[p][po:po + 64, t0:t0 + 512], in0=banks[bo][po:po + 64, :], in1=rec2[r][po:po + 64, :], op=ALU.mult),
                                        reads=[bbuf[bo], B_rec2[r]], writes=[B_yb[p]])
                                if last_of_chunk:
                                    sc.dma("sp", lambda e: e.dma_start(out=yb_s[c, :, tb0:tb0 + S], in_=yb[p][:]), reads=[B_yb[p]])
                            work.append((front, back))
        LA = 2
        ld_at = {0: chunk_loads[0]}
        for k in range(1, len(chunk_loads)):
            ld_at[chunk_first[k - 1] + LA + 2] = chunk_loads[k]
        for i in range(len(work) + LA):
            if i in ld_at:
                ld_at[i]()
            if i < len(work):
                work[i][0]()
            if i >= LA:
                work[i - LA][1]()
        sc.barrier()
        ar.reset(m0)

    def mlstm_phase():
        m0 = ar.mark()
        L = 128
        NCH = S // L
        scn = ar.tile([4, 2, S], F32, "scn")
        B_scn = Buf()
        sc.dma("sp", lambda e: e.dma_start(out=scn[:], in_=cst2[0:4, :, :]), writes=[B_scn])
        alias = {"ig": 0, "wk": 0, "fg": 1, "lf": 1, "t1": 2, "wi": 2, "t2": 3, "mt": 3, "fl": 3, "bb": 4, "aa": 5, "cm": 6, "rw": 6}
        gt_ = [ar.tile([4, S], F32, "g_%d" % i) for i in range(7)]
        gb_ = [Buf() for _ in range(7)]
        G = {n: gt_[i] for n, i in alias.items()}
        BG = {n: gb_[i] for n, i in alias.items()}
        sm = {n: ar.tile([4, NCH], F32, "s_" + n) for n in ("mlw", "maft", "mprev", "d1", "dec")}
        BS = {n: Buf() for n in sm}
        Rdec = ar.tile([4, 4, NCH], F32, "Rdec"); B_Rdec = Buf()
        LTc = [ar.tile([2, 4, L], F32, "LT%d" % i) for i in range(2)]; RTc = [ar.tile([2, 4, L], F32, "RT%d" % i) for i in range(2)]
        B_LTc = [Buf(), Buf()]; B_RTc = [Buf(), Buf()]
        for i in range(2):
            sc.emit("pool", lambda e, i=i: e.memset(LTc[i][:].rearrange("p h n -> p (h n)"), 1.0), writes=[B_LTc[i]])
            sc.emit("pool", lambda e, i=i: e.memset(RTc[i][:].rearrange("p h n -> p (h n)"), 1.0), writes=[B_RTc[i]])
        tcol = ar.tile([128, NCH, 12], F32, "tcol"); B_tcol = Buf()
        dcol = ar.tile([128, 4, NCH], F32, "dcol"); B_dcol = Buf()
        qk = ar.tile([128, 8, S], BF16, "mqk"); B_qk = Buf()
        vx = ar.tile([128, NCH, 4, 257], BF16, "mvx"); B_vx = Buf()
        gml = ar.tile([128, D], F32, "gml"); B_gml = Buf()
        sc.dma("sp", lambda e: e.dma_start(out=gml[:], in_=g_ml[:].partition_broadcast(128)), writes=[B_gml])
        sc.emit("pool", lambda e: e.memset(vx[:, :, :, 256:257], 1.0), writes=[B_vx])
        Cf = [ar.tile([128, 257], F32, "Cf%d" % h) for h in range(4)]
        Cb = [ar.tile([128, 257], BF16, "Cb%d" % h) for h in range(4)]
        B_Cf = [Buf() for _ in range(4)]; B_Cb = [Buf() for _ in range(4)]
        Pm = [ar.tile([128, 128], F32, "Pm%d" % i) for i in range(3)]; B_Pm = [Buf() for _ in range(3)]
        Sm = [ar.tile([128, 128], BF16, "Sm%d" % i) for i in range(3)]; B_Sm = [Buf() for _ in range(3)]
        kt = [ar.tile([128, 128], BF16, "kt%d" % i) for i in range(3)]; B_kt = [Buf() for _ in range(3)]
        vp = [ar.tile([128, 257], BF16, "vp%d" % i) for i in range(3)]; B_vp = [Buf() for _ in range(3)]
        B_s0 = [Buf() for _ in range(3)]; B_s1 = [Buf() for _ in range(3)]; B_s5 = [Buf() for _ in range(3)]
        iS = [ar.tile([128, 257], F32, "iS%d" % i) for i in range(2)]; B_iS = [Buf(), Buf()]
        tot = [ar.tile([128, 4, 257], F32, "tot%d" % i) for i in range(2)]; B_tot = [Buf(), Buf()]
        hn = ar.tile([128, 4, 256], F32, "hn"); B_hn = Buf()
        sqh = ar.tile([128, 4, 256], F32, "sqh"); B_sqh = Buf()
        sm4 = [ar.tile([128, 4], F32, "sm4_%d" % i) for i in range(4)]; B_sm4 = [Buf() for _ in range(4)]
        om = [ar.tile([128, D], BF16, "om%d" % i) for i in range(2)]; B_om = [Buf(), Buf()]
        yat = ar.tile([128, D], BF16, "yat"); B_yat = Buf()
        yaT = [ar.tile([128, 8, L], BF16, "yaT%d" % i) for i in range(2)]; B_yaT = [Buf(), Buf()]
        v3 = lambda t_: t_[:].rearrange("p (c l) -> p c l", l=L)
        ii = 0
        for b in range(BL):
            tb0 = b * S
            sc.dma("sp", lambda e, tb0=tb0: e.dma_start(out=G["ig"][:], in_=gm_s[0, :, tb0:tb0 + S]), writes=[BG["ig"]])
            sc.dma("sp", lambda e, tb0=tb0: e.dma_start(out=G["fg"][:], in_=gm_s[1, :, tb0:tb0 + S]), writes=[BG["fg"]])
            sc.dma("sp", lambda e, tb0=tb0: e.dma_start(out=qk[:], in_=qkm_s[:, :, tb0:tb0 + S].rearrange("c p n -> p c n")), writes=[B_qk])
            for h in range(4):
                sc.dma("sp", lambda e, tb0=tb0, h=h: e.dma_start(
                    out=vx[:, :, h, 0:256], in_=vm_s[tb0:tb0 + S, h * 256:(h + 1) * 256].rearrange("(k p) d -> p k d", p=128)), writes=[B_vx])
            logsigmoid(4, G["fg"], G["lf"], G["t1"], G["t2"], BG["fg"], BG["lf"], BG["t1"], BG["t2"])
            sc.emit("dve", lambda e: e.tensor_tensor_scan(out=G["bb"][:], data0=scn[0:4, 0, :], data1=G["lf"][:], initial=0.0, op0=ALU.mult, op1=ALU.add),
                    reads=[B_scn, BG["lf"]], writes=[BG["bb"]])
            sc.emit("dve", lambda e: e.tensor_tensor(out=G["aa"][:], in0=G["ig"][:], in1=G["bb"][:], op=ALU.subtract),
                    reads=[BG["ig"], BG["bb"]], writes=[BG["aa"]])
            sc.emit("dve", lambda e: e.tensor_tensor_scan(out=G["cm"][:], data0=scn[0:4, 1, :], data1=G["aa"][:], initial=0.0, op0=ALU.add, op1=ALU.max),
                    reads=[B_scn, BG["aa"]], writes=[BG["cm"]])
            blast = G["bb"][:, L - 1:S:L]
            cml = G["cm"][:, L - 1:S:L]
            sc.emit("dve", lambda e: e.tensor_tensor(out=sm["mlw"][:], in0=blast, in1=cml, op=ALU.add), reads=[BG["bb"], BG["cm"]], writes=[BS["mlw"]])
            sc.emit("dve", lambda e: e.tensor_tensor_scan(out=sm["maft"][:], data0=blast, data1=sm["mlw"][:], initial=0.0, op0=ALU.add, op1=ALU.max),
                    reads=[BG["bb"], BS["mlw"]], writes=[BS["maft"]])
            sc.emit("dve", lambda e: e.memset(sm["mprev"][:, 0:1], 0.0), writes=[BS["mprev"]])
            sc.emit("dve", lambda e: e.tensor_copy(out=sm["mprev"][:, 1:NCH], in_=sm["maft"][:, 0:NCH - 1]), reads=[BS["maft"]], writes=[BS["mprev"]])
            sc.emit("dve", lambda e: e.tensor_tensor(out=v3(G["t1"]), in0=v3(G["bb"]), in1=sm["mprev"][:].unsqueeze(2).to_broadcast([4, NCH, L]), op=ALU.add),
                    reads=[BG["bb"], BS["mprev"]], writes=[BG["t1"]])
            sc.emit("dve", lambda e: e.tensor_tensor(out=G["t2"][:], in0=G["bb"][:], in1=G["cm"][:], op=ALU.add), reads=[BG["bb"], BG["cm"]], writes=[BG["t2"]])
            sc.emit("dve", lambda e: e.tensor_tensor(out=G["mt"][:], in0=G["t1"][:], in1=G["t2"][:], op=ALU.max), reads=[BG["t1"], BG["t2"]], writes=[BG["mt"]])
            sc.emit("dve", lambda e: e.tensor_tensor(out=G["rw"][:], in0=G["bb"][:], in1=G["mt"][:], op=ALU.subtract), reads=[BG["bb"], BG["mt"]], writes=[BG["rw"]])
            sc.emit("dve", lambda e: e.tensor_tensor(out=G["wi"][:], in0=G["t1"][:], in1=G["mt"][:], op=ALU.subtract), reads=[BG["t1"], BG["mt"]], writes=[BG["wi"]])
            sc.emit("act", lambda e: e.activation(out=G["wi"][:], in_=G["wi"][:], func=AF.Exp), reads=[BG["wi"]], writes=[BG["wi"]])
            sc.emit("act", lambda e: e.activation(out=G["fl"][:], in_=G["mt"][:], func=AF.Exp, scale=-1.0), reads=[BG["mt"]], writes=[BG["fl"]])
            sc.emit("dve", lambda e: e.tensor_tensor(out=sm["d1"][:], in0=blast, in1=sm["maft"][:], op=ALU.subtract), reads=[BG["bb"], BS["maft"]], writes=[BS["d1"]])
            sc.emit("dve", lambda e: e.tensor_tensor(out=v3(G["wk"]), in0=v3(G["aa"]), in1=sm["d1"][:].unsqueeze(2).to_broadcast([4, NCH, L]), op=ALU.add),
                    reads=[BG["aa"], BS["d1"]], writes=[BG["wk"]])
            sc.emit("act", lambda e: e.activation(out=G["wk"][:], in_=G["wk"][:], func=AF.Exp), reads=[BG["wk"]], writes=[BG["wk"]])
            sc.emit("dve", lambda e: e.tensor_tensor(out=sm["dec"][:], in0=sm["d1"][:], in1=sm["mprev"][:], op=ALU.add), reads=[BS["d1"], BS["mprev"]], writes=[BS["dec"]])
            sc.emit("act", lambda e: e.activation(out=sm["dec"][:], in_=sm["dec"][:], func=AF.Exp), reads=[BS["dec"]], writes=[BS["dec"]])
            sc.emit("dve", lambda e: e.tensor_tensor(out=Rdec[:], in0=sm["dec"][:].unsqueeze(1).to_broadcast([4, 4, NCH]),
                                                     in1=ident[0:4, 0:4].unsqueeze(2).to_broadcast([4, 4, NCH]), op=ALU.mult),
                    reads=[BS["dec"], B_const], writes=[B_Rdec])
            for ch in range(NCH):
                for qi, nm in enumerate(("wk", "wi", "fl")):
                    sc.emit("pe", lambda e, ch=ch, qi=qi, nm=nm: e.matmul(
                        banks[6][:, ch * 12 + qi * 4: ch * 12 + qi * 4 + 4], lhsT=G[nm][0:4, ch * L:(ch + 1) * L], rhs=ident[0:4, 0:4],
                        start=True, stop=True), reads=[BG[nm], B_const], writes=[bbuf[6]], inc=(qi == 2))
            sc.emit("dve", lambda e: e.tensor_copy(out=tcol[:].rearrange("p c q -> p (c q)"), in_=banks[6][:, 0:NCH * 12]), reads=[bbuf[6]], writes=[B_tcol])
            sc.emit("pe", lambda e: e.matmul(banks[6][:, 256:256 + 4 * NCH], lhsT=ones[0:4, :], rhs=Rdec[:].rearrange("k h c -> k (h c)"),
                                             start=True, stop=True), reads=[B_Rdec, B_const], writes=[bbuf[6]])
            sc.emit("dve", lambda e: e.tensor_copy(out=dcol[:].rearrange("p h c -> p (h c)"), in_=banks[6][:, 256:256 + 4 * NCH]), reads=[bbuf[6]], writes=[B_dcol])
            for h in range(4):
                sc.emit("pool", lambda e, h=h: e.memset(Cf[h][:], 0.0), writes=[B_Cf[h]])
                sc.emit("pool", lambda e, h=h: e.memset(Cb[h][:], 0.0), writes=[B_Cb[h]])
            def post(ch, tp, tb0):
                tt = tot[tp]
                sc.emit("act", lambda e, tt=tt: e.activation(out=sm4[0][:], in_=tt[:, :, 256], func=AF.Abs),
                        reads=[B_tot[tp]], writes=[B_sm4[0]])
                sc.emit("dve", lambda e, ch=ch: e.tensor_tensor(out=sm4[0][:], in0=sm4[0][:], in1=tcol[:, ch, 8:12], op=ALU.max),
                        reads=[B_sm4[0], B_tcol], writes=[B_sm4[0]])
                sc.emit("dve", lambda e: e.reciprocal(out=sm4[1][:], in_=sm4[0][:]), reads=[B_sm4[0]], writes=[B_sm4[1]])
                sc.emit("dve", lambda e, tt=tt: e.tensor_tensor(out=hn[:], in0=tt[:, :, 0:256], in1=sm4[1][:].unsqueeze(2).to_broadcast([128, 4, 256]), op=ALU.mult),
                        reads=[B_tot[tp], B_sm4[1]], writes=[B_hn])
                sc.emit("act", lambda e: e.activation(out=sqh[:].rearrange("p h d -> p (h d)"), in_=hn[:].rearrange("p h d -> p (h d)"), func=AF.Square),
                        reads=[B_hn], writes=[B_sqh])
                sc.emit("dve", lambda e: e.tensor_reduce(out=sm4[2][:], in_=sqh[:], axis=AX.X, op=ALU.add), reads=[B_sqh], writes=[B_sm4[2]])
                sc.emit("act", lambda e: e.activation(out=sm4[2][:], in_=sm4[2][:], func=AF.Sqrt, scale=1.0 / 256, bias=epsb[:]),
                        reads=[B_sm4[2], B_const2], writes=[B_sm4[2]])
                sc.emit("dve", lambda e: e.reciprocal(out=sm4[3][:], in_=sm4[2][:]), reads=[B_sm4[2]], writes=[B_sm4[3]])
                sc.emit("dve", lambda e: e.tensor_tensor(out=hn[:], in0=hn[:], in1=sm4[3][:].unsqueeze(2).to_broadcast([128, 4, 256]), op=ALU.mult),
                        reads=[B_hn, B_sm4[3]], writes=[B_hn])
                sc.emit("dve", lambda e: e.tensor_tensor(out=hn[:].rearrange("p h d -> p (h d)"), in0=hn[:].rearrange("p h d -> p (h d)"), in1=gml[:], op=ALU.mult),
                        reads=[B_hn, B_gml], writes=[B_hn])
                sc.emit("dve", lambda e, tp=tp: e.tensor_tensor(out=yat[:], in0=hn[:].rearrange("p h d -> p (h d)"), in1=om[tp][:], op=ALU.mult),
                        reads=[B_hn, B_om[tp]], writes=[B_yat])
                for half in range(2):
                    bk = 6 + half
                    for e4 in range(4):
                        ec = half * 4 + e4
                        sc.emit("pe", lambda e, bk=bk, e4=e4, ec=ec: e.matmul(
                            banks[bk][:, e4 * L:(e4 + 1) * L], lhsT=yat[:, ec * 128:(ec + 1) * 128], rhs=identb[:], start=True, stop=True),
                            reads=[B_yat, B_cb], writes=[bbuf[bk]], inc=(e4 == 3))
                    if half == 0:
                        sc.emit("act", lambda e, bk=bk, tp=tp, half=half: e.copy(
                            out=yaT[tp][:, half * 4:half * 4 + 4, :].rearrange("p c n -> p (c n)"), in_=banks[bk][:, :]),
                            reads=[bbuf[bk]], writes=[B_yaT[tp]])
                    else:
                        sc.emit("dve", lambda e, bk=bk, tp=tp, half=half: e.tensor_copy(
                            out=yaT[tp][:, half * 4:half * 4 + 4, :].rearrange("p c n -> p (c n)"), in_=banks[bk][:, :]),
                            reads=[bbuf[bk]], writes=[B_yaT[tp]])
                sc.dma("sp", lambda e, tp=tp, ch=ch, tb0=tb0: e.dma_start(
                    out=ya_s[:, :, tb0 + ch * L: tb0 + (ch + 1) * L].rearrange("c p n -> p c n"), in_=yaT[tp][:]), reads=[B_yaT[tp]])
            items = []

            def chunk_loads(ch, tb0=tb0):
                tp = ch % 2
                cs = slice(ch * L, (ch + 1) * L)
                sc.dma("sp", lambda e: e.dma_start(out=om[tp][:], in_=om_s[tb0 + ch * L: tb0 + (ch + 1) * L, :]), writes=[B_om[tp]])
                for h in range(4):
                    sc.dma("sp", lambda e, h=h: e.dma_start(out=LTc[tp][0:1, h, :], in_=G["aa"][h:h + 1, cs]), reads=[BG["aa"]], writes=[B_LTc[tp]])
                    sc.dma("sp", lambda e, h=h: e.dma_start(out=RTc[tp][1:2, h, :], in_=G["rw"][h:h + 1, cs]), reads=[BG["rw"]], writes=[B_RTc[tp]])

            chunk_loads(0)
            for ch in range(NCH):
                cs = slice(ch * L, (ch + 1) * L)
                tp = ch % 2
                for h in range(4):
                    i3 = ii % 3; i2 = ii % 2; ii += 1
                    qTc = qk[:, h, cs]; kTc = qk[:, 4 + h, cs]
                    sl = slice(i3 * L, (i3 + 1) * L)

                    def stA(ch=ch, h=h, tp=tp, i3=i3, qTc=qTc, kTc=kTc):
                        if h == 2 and ch + 1 < NCH:
                            chunk_loads(ch + 1)
                        bka = (0, 1, 5)[i3]
                        Ba = bbuf[bka]
                        sc.emit("pe", lambda e: e.matmul(banks[bka][:, 0:L], lhsT=kTc, rhs=qTc, start=True, stop=True),
                                reads=[B_qk], writes=[Ba], inc=False)
                        sc.emit("pe", lambda e: e.matmul(banks[bka][:, L:2 * L], lhsT=LTc[tp][0:2, h, :], rhs=RTc[tp][0:2, h, :], start=True, stop=False),
                                reads=[B_LTc[tp], B_RTc[tp]], writes=[Ba], inc=False)
                        sc.emit("pe", lambda e: e.matmul(banks[bka][:, L:2 * L], lhsT=identb[:], rhs=maskb[:], start=False, stop=True),
                                reads=[B_cb], writes=[Ba], inc=False)
                        sc.emit("pe", lambda e: e.matmul(banks[bka][:, 2 * L:3 * L], lhsT=kTc, rhs=identb[:], start=True, stop=True),
                                reads=[B_qk, B_cb], writes=[Ba])
                        sc.emit("act", lambda e: e.activation(out=Pm[i3][:], in_=banks[bka][:, L:2 * L], func=AF.Exp), reads=[Ba], writes=[B_Pm[i3]])
                        sc.emit("dve", lambda e: e.tensor_tensor(out=Sm[i3][:], in0=banks[bka][:, 0:L], in1=Pm[i3][:], op=ALU.mult),
                                reads=[Ba, B_Pm[i3]], writes=[B_Sm[i3]])
                        sc.emit("act", lambda e: e.activation(out=kt[i3][:], in_=banks[bka][:, 2 * L:3 * L], func=AF.Identity, scale=tcol[:, ch, h:h + 1]),
                                reads=[Ba, B_tcol], writes=[B_kt[i3]])

                    def stB(ch=ch, h=h, tp=tp, i3=i3, i2=i2, qTc=qTc, tb0=tb0):
                        sc.emit("pe", lambda e: e.matmul(banks[4][:, 0:257], lhsT=kt[i3][:], rhs=vx[:, ch, h, :], start=True, stop=True),
                                reads=[B_kt[i3], B_vx], writes=[bbuf[4]])
                        sc.emit("pe", lambda e: e.matmul(banks[3][:, 0:257], lhsT=qTc, rhs=Cb[h][:], start=True, stop=True),
                                reads=[B_qk, B_Cb[h]], writes=[bbuf[3]])
                        sc.emit("pe", lambda e: e.matmul(banks[2][:, 0:257], lhsT=Sm[i3][:], rhs=vx[:, ch, h, :], start=True, stop=True),
                                reads=[B_Sm[i3], B_vx], writes=[bbuf[2]])
                        sc.emit("dve", lambda e: e.scalar_tensor_tensor(
                            out=Cf[h][:], in0=Cf[h][:], scalar=dcol[:, h, ch:ch + 1], in1=banks[4][:, 0:257], op0=ALU.mult, op1=ALU.add),
                            reads=[B_Cf[h], B_dcol, bbuf[4]], writes=[B_Cf[h]])
                        sc.emit("pool", lambda e: e.tensor_copy(out=Cb[h][:], in_=Cf[h][:]), reads=[B_Cf[h]], writes=[B_Cb[h]])
                        sc.emit("act", lambda e: e.activation(out=iS[i2][:], in_=banks[3][:, 0:257], func=AF.Copy, scale=tcol[:, ch, 4 + h:5 + h]),
                                reads=[bbuf[3], B_tcol], writes=[B_iS[i2]])
                        sc.emit("dve", lambda e: e.tensor_tensor(out=tot[tp][:, h, :], in0=banks[2][:, 0:257], in1=iS[i2][:], op=ALU.add),
                                reads=[bbuf[2], B_iS[i2]], writes=[B_tot[tp]])
                        if h == 3:
                            post(ch, tp, tb0)
                    items.append((stA, stB))
            LAG = 2
            for i in range(len(items) + LAG):
                if i < len(items):
                    items[i][0]()
                if i >= LAG:
                    items[i - LAG][1]()
        sc.barrier()
        ar.reset(m0)

    def merge_phase():
        NT = 512
        m0 = ar.mark()
        W = {}
        BW = {}
        for n, d_ in (("a", w_ba), ("b", w_bb), ("o", w_o)):
            W[n] = ar.tile([128, 8, D], BF16, "w_" + n)
            BW[n] = Buf()
            sc.dma("pool", lambda e, n=n, d_=d_: e.dma_start(out=W[n][:], in_=d_.rearrange("(c p) n -> p c n", p=128)), writes=[BW[n]])
        T5 = {}
        B5 = {}
        srcs = {"ya": ya_s, "yb": yb_s, "ga": ga_s, "gb": gb_s}
        for n in srcs:
            T5[n] = [ar.tile([128, 8, NT], BF16, "m_%s%d" % (n, i)) for i in range(2)]
            B5[n] = [Buf(), Buf()]
        xT = [ar.tile([128, 8, NT], F32, "m_x%d" % i) for i in range(2)]; B_x = [Buf(), Buf()]
        mg = ar.tile([128, 8, NT], BF16, "m_mg"); B_mg = [Buf() for _ in range(8)]
        m1 = [ar.tile([128, NT], F32, "m_m1%d" % i) for i in range(2)]; B_m1 = [Buf(), Buf()]
        m2 = [ar.tile([128, NT], F32, "m_m2%d" % i) for i in range(2)]; B_m2 = [Buf(), Buf()]
        k = 0
        for it in range(T // NT):
            p = it % 2
            b = (it * NT) // S
            t0 = it * NT
            for n in srcs:
                sc.dma("sp", lambda e, n=n, p=p, t0=t0: e.dma_start(out=T5[n][p][:], in_=srcs[n][:, :, t0:t0 + NT].rearrange("c p n -> p c n")),
                       writes=[B5[n][p]])
            sc.dma("sp", lambda e, p=p, t0=t0: e.dma_start(out=xT[p][:], in_=x1s[:, :, t0:t0 + NT].rearrange("c p n -> p c n")), writes=[B_x[p]])
            for c in range(8):
                i2 = k % 2; k += 1
                for (bk, wn, yn) in ((0, "a", "ya"), (1, "b", "yb")):
                    bk = bk + 2 * i2
                    for ec in range(8):
                        sc.emit("pe", lambda e, bk=bk, wn=wn, yn=yn, ec=ec, c=c, p=p: e.matmul(
                            banks[bk][:, :], lhsT=W[wn][:, ec, c * 128:(c + 1) * 128], rhs=T5[yn][p][:, ec, :], start=(ec == 0), stop=(ec == 7)),
                            reads=[BW[wn], B5[yn][p]], writes=[bbuf[bk]], inc=(ec == 7))
                sc.emit("dve", lambda e, i2=i2, c=c, p=p: e.tensor_tensor(out=m1[i2][:], in0=banks[2 * i2][:, :], in1=T5["ga"][p][:, c, :], op=ALU.mult),
                        reads=[bbuf[2 * i2], B5["ga"][p]], writes=[B_m1[i2]])
                sc.emit("dve", lambda e, i2=i2, c=c, p=p: e.tensor_tensor(out=m2[i2][:], in0=banks[1 + 2 * i2][:, :], in1=T5["gb"][p][:, c, :], op=ALU.mult),
                        reads=[bbuf[1 + 2 * i2], B5["gb"][p]], writes=[B_m2[i2]])
                sc.emit("pool", lambda e, i2=i2, c=c: e.tensor_tensor(out=mg[:, c, :], in0=m1[i2][:], in1=m2[i2][:], op=ALU.add),
                        reads=[B_m1[i2], B_m2[i2]], writes=[B_mg[c]])
            for c in range(8):
                bk = 4 + (c % 2)
                for ec in range(8):
                    sc.emit("pe", lambda e, bk=bk, ec=ec, c=c: e.matmul(
                        banks[bk][:, :], lhsT=W["o"][:, ec, c * 128:(c + 1) * 128], rhs=mg[:, ec, :], start=(ec == 0), stop=(ec == 7)),
                        reads=[BW["o"], B_mg[ec]], writes=[bbuf[bk]], inc=(ec == 7))
                sc.emit("dve", lambda e, bk=bk, c=c, p=p, b=b: e.scalar_tensor_tensor(
                    out=xT[p][:, c, :], in0=banks[bk][:, :], scalar=coef[:, 5, c, b:b + 1], in1=xT[p][:, c, :], op0=ALU.mult, op1=ALU.add),
                    reads=[bbuf[bk], B_coef, B_x[p]], writes=[B_x[p]])
            sc.dma("pool", lambda e, p=p, t0=t0: e.dma_start(out=x1s[:, :, t0:t0 + NT].rearrange("c p n -> p c n"), in_=xT[p][:]), reads=[B_x[p]])
        sc.barrier()
        ar.reset(m0)

    pmark = ar.mark()
    if debug == ("ffn1only",):
        ffn_phase(0, w1_in, w1_out, x_in, y_out, None, None)
    else:
        ffn_phase(0, w1_in, w1_out, x_in, None, None, x1s)
        mix_phase_fm()
        mix_phase_tm()
        mlstm_phase()
        fox_phase()
        if not (debug and "nomerge" in debug):
            merge_phase()
            ffn_phase(2, w2_in, w2_out, None, y_out, x1s, None)

    for ev in out_evs:
        sc.need("sp", ev)

    with nc.Block() as block:
        sc.run(block)
    return nc


def make_consts():
    c = np.zeros((128, 640), np.float32)
    c[:, 0:128] = np.eye(128, dtype=np.float32)
    s = np.arange(128)[:, None]
    t = np.arange(128)[None, :]
    c[:, 128:256] = np.where(s <= t, 0.0, NEG).astype(np.float32)
    c[:, 256:384] = 1.0
    c[:, 384:512] = ((s // 64) == (t // 64)).astype(np.float32)
    c[:, 512:640] = np.eye(128, dtype=np.float32)
    return c


def make_consts2():
    c = np.zeros((16, 2, S), np.float32)
    c[:, 0, :] = 1.0
    c[:, 0, ::128] = 0.0
    c[:, 1, ::128] = NEG
    return c


_NC_CACHE = {}


def kernel(**inputs):
    debug = inputs.pop("_debug", None)
    ncores = inputs.pop("_ncores", NCORES)
    key = debug
    if key not in _NC_CACHE:
        _NC_CACHE[key] = build_program(debug)
    nc = _NC_CACHE[key]
    consts = make_consts()
    f = lambda a: np.ascontiguousarray(np.asarray(a, dtype=np.float32))
    shared = {
        "w_ada": f(inputs["w_ada"][0]), "b_ada": f(inputs["b_ada"][0]),
        "ffn1_norm_g": f(inputs["ffn1_norm_g"][0]), "ffn1_w_in": f(inputs["ffn1_w_in"][0]),
        "ffn1_w_out": f(inputs["ffn1_w_out"][0]), "mix_norm_g": f(inputs["mix_norm_g"][0]),
        "w_mix": f(inputs["w_mix"][0]), "b_mix": f(inputs["b_mix"][0]),
        "conv_w": f(inputs["conv_w"][0]), "conv_b": f(inputs["conv_b"][0]),
        "mlstm_norm_g": f(inputs["mlstm_norm_g"][0]).reshape(-1),
        "fox_q_norm_g": f(inputs["fox_q_norm_g"][0]).reshape(-1),
        "fox_k_norm_g": f(inputs["fox_k_norm_g"][0]).reshape(-1),
        "w_branch_a": f(inputs["w_branch_a"][0]), "w_branch_b": f(inputs["w_branch_b"][0]),
        "w_out": f(inputs["w_out"][0]), "ffn2_norm_g": f(inputs["ffn2_norm_g"][0]),
        "ffn2_w_in": f(inputs["ffn2_w_in"][0]), "ffn2_w_out": f(inputs["ffn2_w_out"][0]),
        "consts": consts,
        "consts2": make_consts2(),
    }
    x = np.asarray(inputs["x"], dtype=np.float32)
    c = np.asarray(inputs["c"], dtype=np.float32)
    in_maps = []
    for i in range(ncores):
        m = dict(shared)
        m["x"] = np.ascontiguousarray(x[i * BL:(i + 1) * BL].reshape(T, D))
        m["c"] = np.ascontiguousarray(c[i * BL:(i + 1) * BL])
        in_maps.append(m)
    res = run_bass_kernel_spmd(nc, in_maps, core_ids=list(range(ncores)))
    if debug:
        return res.results
    out = np.concatenate([r["y"].reshape(BL, S, D) for r in res.results], axis=0)
    return out.astype(np.float32)
```
